# Optimizing a Trainium2 kernel written in Bass

```python
import jax, jax.numpy as jnp
from jax import lax
import numpy as np

D_MODEL = 1024
BATCH = 4
SEQ = 8192
DEPTH = 1
DEC_BATCH = 32
DEC_SEQ = 32
PAST_LEN = 2048

CHUNK = 64
D_MIX = 2 * D_MODEL
SSD_WIDTH = D_MODEL
SSD_HEAD_DIM = 64
SSD_HEADS = SSD_WIDTH // SSD_HEAD_DIM
SSD_GROUPS = 2
SSD_STATE = 128
SSD_CONV_W = 4
SSD_CHUNK = CHUNK
CONV_DIM = SSD_WIDTH + 2 * SSD_GROUPS * SSD_STATE
GMLP_WIDTH = D_MIX - SSD_WIDTH
GMLP_GROUPS = 8
GMLP_GROUP_DIM = GMLP_WIDTH // GMLP_GROUPS
GMLP_CHUNK = 128
D_IN_PROJ = SSD_WIDTH + CONV_DIM + SSD_HEADS + 3 * GMLP_WIDTH
EPS = 1e-6

kernel_name = "hybrid_ssd_gmlp_stream_step"


def rms_norm(x, g):
    xf = x.astype(jnp.float32)
    y = xf * lax.rsqrt(jnp.mean(xf * xf, axis=-1, keepdims=True) + EPS)
    return (y * g.astype(jnp.float32)).astype(x.dtype)


def layer_norm(x, g, b):
    xf = x.astype(jnp.float32)
    mu = jnp.mean(xf, axis=-1, keepdims=True)
    xc = xf - mu
    y = xc * lax.rsqrt(jnp.mean(xc * xc, axis=-1, keepdims=True) + EPS)
    return (y * g.astype(jnp.float32) + b.astype(jnp.float32)).astype(x.dtype)


def causal_conv(xbc, prev, w, b):
    xp = jnp.concatenate([prev.astype(xbc.dtype), xbc], axis=1)
    L = xbc.shape[1]
    out = b
    for k in range(SSD_CONV_W):
        out = out + w[k] * xp[:, k:k + L]
    return jax.nn.silu(out), xp[:, -(SSD_CONV_W - 1):]


def ssd_scan(x, dt, a, bm, cm, h0):
    f32 = jnp.float32
    bsz, L, H, P = x.shape
    G, N = SSD_GROUPS, SSD_STATE
    R = H // G
    lc = min(L, SSD_CHUNK)
    nc = L // lc
    xs = x.astype(f32).reshape(bsz, nc, lc, G, R, P)
    dts = dt.astype(f32).reshape(bsz, nc, lc, G, R)
    bs = bm.astype(f32).reshape(bsz, nc, lc, G, N)
    cs = cm.astype(f32).reshape(bsz, nc, lc, G, N)
    a_cum = jnp.cumsum(dts * a.astype(f32).reshape(G, R), axis=2)
    mask = jnp.tril(jnp.ones((lc, lc), dtype=bool))
    seg = a_cum[:, :, :, None] - a_cum[:, :, None, :]
    decay = jnp.exp(jnp.where(mask[:, :, None, None], seg, -jnp.inf))
    cb = jnp.einsum('bclgn,bcsgn->bclsg', cs, bs)
    w_ls = cb[..., None] * decay * dts[:, :, None]
    y_diag = jnp.einsum('bclsgr,bcsgrp->bclgrp', w_ls, xs)
    decay_end = jnp.exp(a_cum[:, :, -1:] - a_cum)
    chunk_states = jnp.einsum('bclgn,bclgr,bclgrp->bcgrpn', bs, decay_end * dts, xs)
    chunk_decay = jnp.exp(a_cum[:, :, -1])

    def step(h, inp):
        st, dc = inp
        return dc[..., None, None] * h + st, h

    h_init = h0.astype(f32).reshape(bsz, G, R, P, N)
    h_last, h_prev = lax.scan(step, h_init,
                              (jnp.moveaxis(chunk_states, 1, 0), jnp.moveaxis(chunk_decay, 1, 0)))
    h_prev = jnp.moveaxis(h_prev, 0, 1)
    y_off = jnp.einsum('bclgn,bclgr,bcgrpn->bclgrp', cs, jnp.exp(a_cum), h_prev)
    y = (y_diag + y_off).reshape(bsz, L, H, P)
    return y.astype(x.dtype), h_last.reshape(bsz, H, P, N).astype(x.dtype)


def gmlp_spatial(v, w_s, b_s):
    bsz, L, _ = v.shape
    lc = min(L, GMLP_CHUNK)
    nc = L // lc
    vs = v.reshape(bsz, nc, lc, GMLP_GROUPS, GMLP_GROUP_DIM)
    mask = jnp.tril(jnp.ones((lc, lc), dtype=bool))
    w = jnp.where(mask[None], w_s[:, :lc, :lc], 0).astype(v.dtype)
    mixed = jnp.einsum('gts,bnsgd->bntgd', w, vs) + b_s[:, :lc].T[None, None, :, :, None]
    return mixed.reshape(bsz, L, GMLP_WIDTH)


def hybrid_layer(x, c, conv_prev, ssm_prev, w_ada, b_ada, g_pre, g_post, w_in, conv_w, conv_b,
                 dt_bias, a_log, d_skip, g_v, beta_v, w_s, b_s, w_out):
    bsz, L, _ = x.shape
    mod = jax.nn.silu(c) @ w_ada + b_ada
    shift, scale, gate = jnp.split(mod, 3, axis=-1)
    h = rms_norm(x, g_pre) * (1 + scale[:, None]) + shift[:, None]
    proj = h @ w_in
    i1 = SSD_WIDTH
    i2 = i1 + CONV_DIM
    i3 = i2 + SSD_HEADS
    i4 = i3 + GMLP_WIDTH
    i5 = i4 + GMLP_WIDTH
    z, xbc, dt_raw, u, v, gt = jnp.split(proj, [i1, i2, i3, i4, i5], axis=-1)
    xbc, conv_new = causal_conv(xbc, conv_prev, conv_w, conv_b)
    xs, bm, cm = jnp.split(xbc, [SSD_WIDTH, SSD_WIDTH + SSD_GROUPS * SSD_STATE], axis=-1)
    xs = xs.reshape(bsz, L, SSD_HEADS, SSD_HEAD_DIM)
    dt = jax.nn.softplus(dt_raw + dt_bias)
    a = -jnp.exp(a_log.astype(jnp.float32))
    y, ssm_new = ssd_scan(xs, dt, a,
                          bm.reshape(bsz, L, SSD_GROUPS, SSD_STATE),
                          cm.reshape(bsz, L, SSD_GROUPS, SSD_STATE), ssm_prev)
    y = y + d_skip[:, None] * xs
    ssd_out = y.reshape(bsz, L, SSD_WIDTH) * jax.nn.silu(z)
    v_n = layer_norm(v, g_v, beta_v)
    gmlp_out = jax.nn.silu(gt) * u * gmlp_spatial(v_n, w_s, b_s)
    mix = jnp.concatenate([ssd_out, gmlp_out], axis=-1) @ w_out
    out = x + gate[:, None] * rms_norm(mix, g_post)
    return out, conv_new, ssm_new, v_n


def setup_inputs(seed: int = 0) -> dict:
    key = jax.random.key(seed)
    ks = jax.random.split(key, 24)
    f32 = jnp.float32
    nrm = lambda k, s: jax.random.normal(k, s, dtype=f32)
    dt0 = jnp.exp(jax.random.uniform(ks[10], (DEPTH, SSD_HEADS), dtype=f32,
                                     minval=np.log(1e-3), maxval=np.log(1e-1)))
    return {
        "x_prompt": nrm(ks[0], (BATCH, SEQ, D_MODEL)),
        "x_sample": nrm(ks[1], (DEC_BATCH, DEC_SEQ, D_MODEL)),
        "state_conv": nrm(ks[2], (DEPTH, DEC_BATCH, SSD_CONV_W - 1, CONV_DIM)),
        "state_ssm": 0.1 * nrm(ks[3], (DEPTH, DEC_BATCH, SSD_HEADS, SSD_HEAD_DIM, SSD_STATE)),
        "c_prompt": nrm(ks[4], (BATCH, D_MODEL)),
        "c_sample": nrm(ks[5], (DEC_BATCH, D_MODEL)),
        "w_ada": 0.3 * D_MODEL ** -0.5 * nrm(ks[6], (DEPTH, D_MODEL, 3 * D_MODEL)),
        "b_ada": 0.01 * nrm(ks[7], (DEPTH, 3 * D_MODEL)),
        "g_pre": 1.0 + 0.1 * nrm(ks[8], (DEPTH, D_MODEL)),
        "g_post": 1.0 + 0.1 * nrm(ks[9], (DEPTH, D_MODEL)),
        "w_in": D_MODEL ** -0.5 * nrm(ks[11], (DEPTH, D_MODEL, D_IN_PROJ)),
        "conv_w": SSD_CONV_W ** -0.5 * nrm(ks[12], (DEPTH, SSD_CONV_W, CONV_DIM)),
        "conv_b": 0.01 * nrm(ks[13], (DEPTH, CONV_DIM)),
        "dt_bias": dt0 + jnp.log(-jnp.expm1(-dt0)),
        "a_log": jnp.log(jax.random.uniform(ks[14], (DEPTH, SSD_HEADS), dtype=f32, minval=1.0, maxval=16.0)),
        "d_skip": 1.0 + 0.1 * nrm(ks[15], (DEPTH, SSD_HEADS)),
        "g_v": 1.0 + 0.1 * nrm(ks[16], (DEPTH, GMLP_WIDTH)),
        "beta_v": 0.01 * nrm(ks[17], (DEPTH, GMLP_WIDTH)),
        "w_s": GMLP_CHUNK ** -0.5 * nrm(ks[18], (DEPTH, GMLP_GROUPS, GMLP_CHUNK, GMLP_CHUNK)),
        "b_s": 1.0 + 0.1 * nrm(ks[19], (DEPTH, GMLP_GROUPS, GMLP_CHUNK)),
        "w_out": D_MIX ** -0.5 * nrm(ks[20], (DEPTH, D_MIX, D_MODEL)),
    }


def reference(x_prompt, x_sample, state_conv, state_ssm, c_prompt, c_sample, w_ada, b_ada, g_pre,
              g_post, w_in, conv_w, conv_b, dt_bias, a_log, d_skip, g_v, beta_v, w_s, b_s, w_out):
    yp, ys = x_prompt, x_sample
    conv_p_l, ssm_p_l, conv_s_l, ssm_s_l, v_s_l = [], [], [], [], []
    for l in range(DEPTH):
        wl = (w_ada[l], b_ada[l], g_pre[l], g_post[l], w_in[l], conv_w[l], conv_b[l], dt_bias[l],
              a_log[l], d_skip[l], g_v[l], beta_v[l], w_s[l], b_s[l], w_out[l])
        conv0 = jnp.zeros((yp.shape[0], SSD_CONV_W - 1, CONV_DIM), dtype=yp.dtype)
        ssm0 = jnp.zeros((yp.shape[0], SSD_HEADS, SSD_HEAD_DIM, SSD_STATE), dtype=yp.dtype)
        yp, conv_p, ssm_p, _ = hybrid_layer(yp, c_prompt, conv0, ssm0, *wl)
        ys, conv_s, ssm_s, v_s = hybrid_layer(ys, c_sample, state_conv[l], state_ssm[l], *wl)
        conv_p_l.append(conv_p)
        ssm_p_l.append(ssm_p)
        conv_s_l.append(conv_s)
        ssm_s_l.append(ssm_s)
        v_s_l.append(v_s)
    conv_prompt = jnp.stack(conv_p_l)
    ssm_prompt = jnp.stack(ssm_p_l)
    conv_sample = jnp.stack(conv_s_l)
    ssm_sample = jnp.stack(ssm_s_l)
    gmlp_v_sample = jnp.stack(v_s_l)
    return (yp, ys, conv_prompt, ssm_prompt, conv_sample, ssm_sample, gmlp_v_sample)
```

```python
import numpy as np
from contextlib import ExitStack
import concourse.bass as bass
import concourse.mybir as mybir
from concourse.bass_utils import run_bass_kernel_spmd

F32 = mybir.dt.float32
BF16 = mybir.dt.bfloat16
AF = mybir.ActivationFunctionType
ALU = mybir.AluOpType

D = 1024
NCOL = 5648
C_Z, C_XBC, C_DT, C_U, C_V, C_G = 0, 1024, 2560, 2576, 3600, 4624
EPS = 1e-6


class _Op:
    __slots__ = ("eng", "fn", "reads", "writes", "dma", "idx", "waits", "mile", "mval", "dcount", "cost", "cls")


_DEF_COST = {"pe": 0.3, "act": 0.5, "dve": 0.5, "pool": 1.0, "sp": 0.1}


class Prog:
    ENGS = ("pe", "act", "dve", "pool", "sp")

    def __init__(self, nc):
        self.nc = nc
        self.ops = []

    def op(self, eng, fn, reads=(), writes=(), dma=None, cost=None, cls=None):
        o = _Op()
        o.cls = cls
        o.eng = eng
        o.fn = fn
        r, w = [], list(writes)
        for k in reads:
            if k.startswith("ps:"):
                if k not in w:
                    w.append(k)
            else:
                r.append(k)
        o.reads = tuple(r)
        o.writes = tuple(w)
        o.dma = dma
        o.idx = len(self.ops)
        o.waits = []
        o.mile = False
        o.mval = 0
        o.dcount = 0
        o.cost = cost if cost is not None else (5.5 if dma is not None else _DEF_COST[eng])
        self.ops.append(o)
        return o

    def finalize(self, reorder=True):
        ops = self.ops
        n = len(ops)
        last_w, readers = {}, {}
        preds = [None] * n
        for o in ops:
            i = o.idx
            d = {}
            for k in o.reads:
                j = last_w.get(k)
                if j is not None:
                    d[j] = True
            for k in o.writes:
                j = last_w.get(k)
                if j is not None:
                    d[j] = True
                for j in readers.get(k, ()):
                    if j not in d:
                        d[j] = False
            d.pop(i, None)
            for k in o.reads:
                readers.setdefault(k, []).append(i)
            for k in o.writes:
                last_w[k] = i
                readers[k] = []
            preds[i] = d
        order = list(range(n))
        if reorder:
            order = self._schedule(preds)
        pos = [0] * n
        for p, i in enumerate(order):
            pos[i] = p
        for i in range(n):
            for j in preds[i]:
                assert pos[j] < pos[i], "schedule violates a dependency"
        dma_counts = {}
        for i in order:
            o = ops[i]
            if o.dma is not None:
                c = dma_counts.get(o.dma[0], 0) + 16 * o.dma[1]
                dma_counts[o.dma[0]] = c
                o.dcount = c
        self.dma_counts = dma_counts
        need = [None] * n
        for i in order:
            o = ops[i]
            best = {}
            for j, raw in preds[i].items():
                pj = ops[j]
                if pj.dma is not None:
                    key = ("dma", pj.dma[0])
                    best[key] = max(best.get(key, 0), pj.dcount)
                    continue
                if pj.eng == o.eng and o.eng in ("pe", "sp"):
                    continue
                key = ("eng", pj.eng)
                if key not in best or pos[best[key]] < pos[j]:
                    best[key] = j
            need[i] = best
            for key, j in best.items():
                if key[0] == "eng":
                    ops[j].mile = True
        cnt = {e: 0 for e in self.ENGS}
        for i in order:
            o = ops[i]
            if o.mile:
                cnt[o.eng] += 1
                o.mval = cnt[o.eng]
        seen = {e: {} for e in self.ENGS}
        for i in order:
            o = ops[i]
            ws = []
            for key, v in need[i].items():
                val = ops[v].mval if key[0] == "eng" else v
                if seen[o.eng].get(key, 0) >= val:
                    continue
                seen[o.eng][key] = val
                ws.append((key, val))
            o.waits = ws
        self.order = order
        self._emit()

    def _schedule(self, preds, delta=0.0, WIN=24, seed=0):
        import heapq
        ops = self.ops
        n = len(ops)
        succs = [[] for _ in range(n)]
        npred = [0] * n
        for i in range(n):
            npred[i] = len(preds[i])
            for j in preds[i]:
                succs[j].append(i)
        bl = [0.0] * n
        for i in range(n - 1, -1, -1):
            m = 0.0
            for s_ in succs[i]:
                if bl[s_] > m:
                    m = bl[s_]
            bl[i] = ops[i].cost + m
        if seed:
            rs = np.random.RandomState(seed)
            jit = rs.uniform(0.0, 0.6, size=n)
        else:
            jit = np.zeros(n)
        finish = [0.0] * n
        ready_t = [0.0] * n
        free = {e: 0.0 for e in self.ENGS}
        cand = {e: [] for e in self.ENGS}
        for i in range(n):
            if npred[i] == 0:
                heapq.heappush(cand[ops[i].eng], (0.0, -bl[i], i))
        order = []
        starts = [0.0] * n
        self.sim_starts = starts
        cur_cls = [None]
        while len(order) < n:
            bestc = None
            for e in self.ENGS:
                h = cand[e]
                if not h:
                    continue
                fe = free[e]
                top = heapq.nsmallest(WIN, h)
                ests = []
                for (rt, nb, i) in top:
                    est = rt if rt > fe else fe
                    if e == "act" and ops[i].cls is not None and cur_cls[0] is not None and ops[i].cls != cur_cls[0]:
                        est += 1.3
                    ests.append((est + jit[i], nb, i, rt))
                emin = min(x[0] for x in ests)
                pick = None
                for (est, nb, i, rt) in ests:
                    if est <= emin + delta:
                        key = (nb, est, i)
                        if pick is None or key < pick[2]:
                            pick = ((est - jit[i], nb, i), (rt, nb, i), key)
                if bestc is None or pick[0] < bestc[0]:
                    bestc = (pick[0], e, pick[1])
            (est, nb, i), e, item = bestc
            cand[e].remove(item)
            heapq.heapify(cand[e])
            o = ops[i]
            if e == "act" and o.cls is not None:
                cur_cls[0] = o.cls
            if o.dma is not None:
                free[e] = est + (1.0 if e == "pool" else 0.1)
                finish[i] = est + o.cost
            else:
                free[e] = est + o.cost
                finish[i] = est + o.cost
            order.append(i)
            starts[i] = est
            for s_ in succs[i]:
                lat = 0.0 if (ops[s_].eng == e and o.dma is None) else 0.15
                t = finish[i] + lat
                if t > ready_t[s_]:
                    ready_t[s_] = t
                npred[s_] -= 1
                if npred[s_] == 0:
                    heapq.heappush(cand[ops[s_].eng], (ready_t[s_], -bl[s_], s_))
        self.sim_makespan = max(finish) if n else 0.0
        return order

    def _emit(self):
        nc = self.nc
        with ExitStack() as st:
            esem = {e: st.enter_context(nc.semaphore("s_" + e)) for e in ("pe", "act", "dve", "pool")}
            dsem = {n: st.enter_context(nc.semaphore("d_" + n)) for n in self.dma_counts}
            block = st.enter_context(nc.Block())
            by_eng = {e: [self.ops[i] for i in self.order if self.ops[i].eng == e] for e in self.ENGS}

            def run(engname, engobj):
                for o in by_eng[engname]:
                    for key, val in o.waits:
                        s = esem[key[1]] if key[0] == "eng" else dsem[key[1]]
                        engobj.wait_ge(s, val)
                    res = o.fn(engobj)
                    if o.dma is not None:
                        if not isinstance(res, (list, tuple)):
                            res = [res]
                        assert len(res) == o.dma[1]
                        for r in res:
                            r.then_inc(dsem[o.dma[0]], 16)
                    elif o.mile:
                        if isinstance(res, (list, tuple)):
                            res = res[-1]
                        res.then_inc(esem[o.eng], 1)
                if engname == "sp":
                    for n, c in self.dma_counts.items():
                        engobj.wait_ge(dsem[n], c)

            block.tensor(lambda e: run("pe", e))
            block.scalar(lambda e: run("act", e))
            block.vector(lambda e: run("dve", e))
            block.gpsimd(lambda e: run("pool", e))
            block.sync(lambda e: run("sp", e))


IN_SPECS = None


def build(n_fast_st, n_full_st):
    nc = bass.Bass("TRN2", target_bir_lowering=False)
    NFT, NMT = n_fast_st * 256, n_full_st * 256

    def din(name, shape, dt=F32):
        return nc.dram_tensor(name, list(shape), dt, kind="ExternalInput").ap()

    def dout(name, shape, dt=F32):
        return nc.dram_tensor(name, list(shape), dt, kind="ExternalOutput").ap()

    xp = din("xp", [NFT, D]); xc = din("xc", [NMT, D]); xsm = din("xsm", [128, D])
    flag = din("flag", [128, 1]); cT = din("cT", [128, 8, 5])
    w_ada = din("w_ada", [D, 3 * D]); bada_fm = din("bada_fm", [128, 16]); bada_g = din("bada_g", [5, D])
    gpre_fm = din("gpre_fm", [128, 8]); gpost5 = din("gpost5", [5, D])
    w_in = din("w_in", [D, NCOL]); w_out = din("w_out", [2 * D, D])
    cw_d = din("cw", [128, 12, 4]); cb_d = din("cb", [128, 12])
    dtb_d = din("dtb", [128, 16]); alog_d = din("alog", [128, 16]); Dp_d = din("Dp", [128, 8])
    gvB_d = din("gvB", [128, D]); bvB_d = din("bvB", [128, D]); brow_d = din("brow", [2, D])
    wsT_d = din("wsT", [2, 128, 8, 128]); bs_d = din("bsrow", [2, 1, 8, 128])
    sconv_d = din("sconv", [128, 12, 4, 3]); sssm_d = din("sssm", [4, 128, D])
    cmask_d = din("cmask", [6, 128, 128])
    segsel_d = din("segsel", [128, 4, 128]); sel5_d = din("sel5", [2, 5, 128])

    yc = dout("yc", [NMT, D]); ysm = dout("ysm", [128, D])
    convp_o = dout("convp", [128, 12, 3]); ssmp_o = dout("ssmp", [128, D])
    convs_o = dout("convs", [128, 12, 4, 3]); ssms_o = dout("ssms", [4, 128, D]); vns_o = dout("vns", [128, D])

    P = Prog(nc)
    SCK = ["sc%d" % i for i in range(8)] + ["scm"]
    with ExitStack() as st:
        def sb(name, shape, dt):
            return st.enter_context(nc.sbuf_tensor(name, list(shape), dt))

        def ps(name, shape, dt=F32):
            return st.enter_context(nc.psum_tensor(name, list(shape), dt))

        win = sb("win", [128, 8, NCOL], BF16)
        wout = sb("wout", [128, 16, D], BF16)
        xin0 = sb("xin0", [128, D], F32)
        res = sb("res", [128, D], F32)
        xs = sb("xs", [128, D], BF16)
        hT = sb("hT", [128, 8, 256], BF16)
        XR = [sb("XR0", [128, 2, 259], F32), sb("XR1", [128, 2, 259], F32)]
        accT = sb("acc", [128, 2, 256], F32)
        acc = [accT[:, 0, :], accT[:, 1, :]]
        bufA = sb("bufA", [128, 8, 256], BF16)
        BCf = sb("BCf", [128, 4, 256], BF16)
        bufB = sb("bufB", [128, 8, 256], BF16)
        bufC = sb("bufC", [128, 8, 256], BF16)
        sgt = sb("sgt", [128, 2, 256], BF16)
        M2 = sb("M2", [128, 8, 128], BF16)
        EW = sb("EW0", [128, 8, 128], BF16)
        mCBT = [sb("mCBT0", [128, 2, 128], BF16)] * 2
        LX = [sb("LX0", [128, D], BF16)] * 2
        XDD = [sb("XDD0", [128, D], BF16)] * 2
        ZS = [sb("ZS0", [128, D], BF16)] * 2
        Btm = [sb("Btm0", [128, 256], BF16)] * 2
        hT2 = sb("hT2", [128, 8, 256], BF16)
        bufA2 = sb("bufA2", [128, 8, 256], BF16)
        BCf2 = sb("BCf2", [128, 4, 256], BF16)
        H = sb("H", [128, D], F32)
        Hbf = sb("Hbf", [128, D], BF16)
        vhat = sb("vhat", [128, D], BF16)
        gvB = sb("gvBb", [128, D], BF16)
        tmpo = sb("tmpo", [128, D], F32)
        GG = sb("GG", [128, D], F32)
        WsT = sb("WsT0", [128, 8, 128], BF16)
        Rhl = sb("Rhl", [128, 1, 8, 128], BF16)
        cmF = sb("cmF", [128, 6, 128], F32)
        cmB = sb("cmB", [128, 6, 128], BF16)
        sel5 = sb("sel5_s", [5, 2, 128], F32)
        scT = sb("scT", [128, 8, 5], F32)
        modfm = sb("modfm", [128, 16, 5], F32)
        gm = sb("gm", [128, 8, 5], F32)
        smallc = sb("smallc", [128, 128], F32)
        halo = sb("halo", [128, 12, 4, 3], F32)
        small = sb("small", [128, 256], F32)
        stats = sb("stats", [128, 2, 6], F32)
        segsel = accT[:].rearrange("p a (q n) -> p (a q) n", n=128)
        rowsA = H
        rowsB = res[:].rearrange("p (g n) -> p g n", n=128)
        gr5_d = nc.dram_tensor("gr5_scr", [5, D], F32).ap()
        wss_d = nc.dram_tensor("wss_scr", [128, 8, 128], BF16).ap()
        rhs_d = nc.dram_tensor("rhs_scr", [128, 1, 8, 128], BF16).ap()

        cw = smallc[:, 0:48].rearrange("p (i k) -> p i k", k=4)
        cb = smallc[:, 48:60]
        dtb = smallc[:, 60:76]
        Abc = smallc[:, 76:92]
        Dp = smallc[:, 92:100]
        gpre = smallc[:, 100:108]
        badaf = smallc[:, 108:124]
        mhalf = smallc[:, 124:125]
        flg = smallc[:, 125:126]

        psF = [ps("psF0", [128, 512]), ps("psF1", [128, 512])]
        psTX = ps("psTX", [128, D], BF16)
        psMi = ps("psMi", [128, 512])
        psP = ps("psP", [128, D])
        psQ = ps("psQ", [128, D])
        KF = ["ps:0", "ps:1"]
        KTX, KMI = "ps:2", "ps:3"
        KP, KQ = ["ps:4", "ps:5"], ["ps:6", "ps:7"]

        identB = cmB[:, 0, :]

        def dma(eng, out, in_, key, reads=(), writes=(), n=1):
            P.op(eng, (lambda e: e.dma_start(out=out, in_=in_)), reads=reads, writes=writes, dma=(key, n))

        HK = ["halo%d" % t for t in range(0, 12, 2)]
        dma("sp", cmF[:], cmask_d.rearrange("c p n -> p c n"), "cmF", writes=["cmF"])
        dma("sp", scT[:], cT, "scT", writes=["scT"])
        dma("sp", smallc[:, 0:48], cw_d.rearrange("p i k -> p (i k)"), "sc0", writes=["sc0"])
        dma("sp", smallc[:, 48:60], cb_d, "sc1", writes=["sc1"])
        dma("sp", smallc[:, 60:76], dtb_d, "sc2", writes=["sc2"])
        dma("sp", smallc[:, 76:92], alog_d, "sc3", writes=["sc3"])
        dma("sp", smallc[:, 92:100], Dp_d, "sc4", writes=["sc4"])
        dma("sp", smallc[:, 100:108], gpre_fm, "sc5", writes=["sc5"])
        dma("sp", smallc[:, 108:124], bada_fm, "sc6", writes=["sc6"])
        dma("sp", smallc[:, 125:126], flag, "sc7", writes=["sc7"])
        dma("sp", sel5[:], sel5_d.rearrange("v j n -> j v n"), "sel5", writes=["sel5"])

        P.op("pool", lambda e: e.memset(mhalf, -0.5), writes=["scm"])
        P.op("dve", lambda e: e.tensor_copy(out=cmB[:], in_=cmF[:]), reads=["cmF"], writes=["cmB"])
        P.op("act", lambda e: e.activation(out=scT[:], in_=scT[:], func=AF.Silu), reads=["scT"], writes=["scT"], cls="S")
        P.op("act", lambda e: e.activation(out=Abc, in_=Abc, func=AF.Exp), reads=["sc3"], writes=["sc3"], cls="E")
        P.op("dve", lambda e: e.tensor_scalar(out=Abc, in0=Abc, scalar1=-1.0, scalar2=None, op0=ALU.mult),
             reads=["sc3"], writes=["sc3"])

        pieces = [(C_XBC, C_XBC + 512), (C_XBC + 512, C_XBC + 1024), (C_XBC + 1024, C_DT + 16),
                  (C_Z, C_Z + 512), (C_Z + 512, C_Z + 1024),
                  (C_G, C_G + 512), (C_U, C_U + 512), (C_G + 512, C_G + 1024), (C_U + 512, C_U + 1024),
                  (C_V, C_V + 512), (C_V + 512, C_V + 1024)]
        w_in_v = w_in.rearrange("(k p) n -> p k n", p=128)
        for pi, (c0, c1) in enumerate(pieces[:3]):
            dma("pool", win[:, :, c0:c1], w_in_v[:, :, c0:c1], "win%d" % pi, writes=["win%d" % pi])

        def wkey(col):
            for pi, (c0, c1) in enumerate(pieces):
                if c0 <= col < c1:
                    return "win%d" % pi
            raise ValueError(col)

        wst = wout[:].bitcast(F32)
        stage = [wst[:, 0:8, :], wst[:, 8:16, :]]
        skeys = [["wout0", "wout1"], ["wout2", "wout3"]]
        gate5 = H[0:5, :]
        for pc in range(6):
            wv, sk = stage[pc % 2], skeys[pc % 2]
            dma("sp", wv, w_ada[:, pc * 512:(pc + 1) * 512].rearrange("(k p) n -> p k n", p=128), "wa%d" % (pc % 2), writes=sk)
            if pc < 4:
                for cbk in range(4):
                    blk = pc * 4 + cbk

                    def mm(e, wv=wv, cbk=cbk):
                        r = None
                        for kc in range(8):
                            r = e.matmul(psMi[:, 8 * cbk:8 * cbk + 5], lhsT=wv[:, kc, cbk * 128:(cbk + 1) * 128], rhs=scT[:, kc, :],
                                         start=(kc == 0), stop=(kc == 7))
                        return r
                    P.op("pe", mm, reads=sk + ["scT"], writes=[KMI], cost=3.5)
                    P.op("dve", (lambda e, blk=blk, cbk=cbk: e.tensor_scalar(out=modfm[:, blk, :], in0=psMi[:, 8 * cbk:8 * cbk + 5],
                                                                          scalar1=badaf[:, blk:blk + 1], scalar2=None, op0=ALU.add)),
                         reads=[KMI, *SCK], writes=["modfm"])
            else:
                def mm(e, wv=wv):
                    r = None
                    for kc in range(8):
                        r = e.matmul(psMi[0:5, :], lhsT=scT[:, kc, :], rhs=wv[:, kc, :], start=(kc == 0), stop=(kc == 7))
                    return r
                P.op("pe", mm, reads=sk + ["scT"], writes=[KMI], cost=8.0)
                P.op("dve", (lambda e, pc=pc: e.tensor_copy(out=gate5[:, (pc - 4) * 512:(pc - 3) * 512], in_=psMi[0:5, :])),
                     reads=[KMI], writes=["H"])
        P.op("dve", lambda e: e.tensor_scalar(out=gm[:], in0=modfm[:, 8:16, :], scalar1=1.0, scalar2=None, op0=ALU.add),
             reads=["modfm"], writes=["gm"])
        P.op("dve", lambda e: e.tensor_tensor(out=gm[:], in0=gm[:], in1=gpre[:, :, None].to_broadcast([128, 8, 5]), op=ALU.mult),
             reads=["gm", *SCK], writes=["gm"])
        dma("sp", GG[0:5, :], bada_g, "g5a", writes=["GG"])
        dma("sp", res[0:5, :], gpost5, "g5b", writes=["res"])
        GR5 = tmpo[0:5, :]
        P.op("dve", lambda e: e.tensor_tensor(out=gate5, in0=gate5, in1=GG[0:5, :], op=ALU.add),
             reads=["H", "GG"], writes=["H"])
        P.op("dve", lambda e: e.tensor_tensor(out=GR5, in0=gate5, in1=res[0:5, :], op=ALU.mult),
             reads=["H", "res"], writes=["tmpo"])
        dma("sp", gr5_d, GR5, "g5s", reads=["tmpo"], writes=["gr5d"])

        def build_GG(variant):
            def mm(e):
                r = None
                for cbk in range(2):
                    r = e.matmul(psP[:, cbk * 512:(cbk + 1) * 512], lhsT=sel5[:, variant, :], rhs=GR5[:, cbk * 512:(cbk + 1) * 512],
                                 start=True, stop=True)
                return r
            P.op("pe", mm, reads=["sel5", "tmpo"], writes=KP)
            P.op("act", lambda e: e.copy(out=GG[:], in_=psP[:]), reads=KP, writes=["GG"])

        build_GG(0)

        for pi, (c0, c1) in enumerate(pieces):
            if pi >= 3:
                dma("pool", win[:, :, c0:c1], w_in_v[:, :, c0:c1], "win%d" % pi, writes=["win%d" % pi])
        w_out_v = w_out.rearrange("(k p) n -> p k n", p=128)
        for pi in range(4):
            dma("pool", wout[:, pi * 4:(pi + 1) * 4, :], w_out_v[:, pi * 4:(pi + 1) * 4, :], "wout%d" % pi, writes=["wout%d" % pi])

        for v in (1, 0):
            wv = tmpo[:].rearrange("p (g n) -> p g n", n=128)
            dma("sp", wv, wsT_d[v], "wsT", writes=["tmpo"])
            mk = cmF[:, 1 + 2 * v, :]
            P.op("dve", (lambda e, wv=wv, mk=mk: e.tensor_tensor(out=WsT[:], in0=wv, in1=mk[:, None, :].to_broadcast([128, 8, 128]),
                                                                 op=ALU.mult)), reads=["tmpo", "cmF"], writes=["WsT0"])
            dma("sp", rowsB[33:34, :, :], bs_d[v], "bsr", writes=["res"])
            dma("sp", rowsA[32:34, :], brow_d, "brr", writes=["H"])
            for cbk in range(2):
                P.op("pe", (lambda e, cbk=cbk: e.matmul(psMi[32:33, :], lhsT=cmB[:, 5, 0:1], rhs=WsT[:, cbk * 4:(cbk + 1) * 4, :],
                                                        start=True, stop=True, tile_position=(0, 32))),
                     reads=["cmB", "WsT0"], writes=[KMI])
                P.op("act", (lambda e, cbk=cbk: e.copy(out=rowsB[32:33, cbk * 4:(cbk + 1) * 4, :],
                                                       in_=psMi[32:33, :].rearrange("p (g t) -> p g t", g=4))),
                     reads=[KMI], writes=["res"])

            def mmr(e):
                r = None
                for g in range(8):
                    r = e.matmul(psQ[:, g * 128:(g + 1) * 128], lhsT=rowsA[32:34, g * 128:(g + 1) * 128], rhs=rowsB[32:34, g, :],
                                 start=True, stop=True)
                return r
            P.op("pe", mmr, reads=["H", "res"], writes=KQ)
            P.op("act", lambda e: e.copy(out=Rhl[:, 0, :, :], in_=psQ[:].rearrange("p (g t) -> p g t", g=8)), reads=KQ, writes=["Rhl"])
            if v == 1:
                dma("sp", wss_d, WsT[:], "wsss", reads=["WsT0"], writes=["wssd"])
                dma("sp", rhs_d, Rhl[:], "rhss", reads=["Rhl"], writes=["rhsd"])
        dma("sp", tmpo[:], gvB_d, "gvl", writes=["tmpo"])
        P.op("dve", lambda e: e.tensor_copy(out=gvB[:], in_=tmpo[:]), reads=["tmpo"], writes=["gvB"])

        P.op("pool", lambda e: e.memset(H[:], 0.0), writes=["H"])
        P.op("pool", lambda e: e.memset(Hbf[:], 0.0), writes=["Hbf"])
        P.op("pool", lambda e: e.memset(halo[:], 0.0), writes=HK)

        flags = set()
        pair_i = [0]
        conv_i = [0]

        def next_slot():
            s_ = pair_i[0] % 2
            pair_i[0] += 1
            return s_

        def wait(*fl):
            while not all(f in flags for f in fl):
                yield

        BS0 = dict(hT=hT, hTk="hT", bufA=bufA, bufAk="bufA", BCf=BCf, BCfk="BCf")
        BS1 = dict(hT=hT2, hTk="hT2", bufA=bufA2, bufAk="bufA2", BCf=BCf2, BCfk="BCf2")

        def modeinfo(mode, S):
            smp = mode == "sample"
            v = 1 if smp else 0
            TS = 128 * S
            nseg, Ls = (4, 32) if smp else (1, TS)
            return smp, v, TS, nseg, Ls

        def gen_A(mode, xsrc, S, bs, pre=(), tag=None):
            smp, v, TS, nseg, Ls = modeinfo(mode, S)
            hT_, hTk = bs["hT"], bs["hTk"]
            for j in range(S):
                dma("sp", xin0[:], xsrc[j * 128:(j + 1) * 128, :], "xl0", writes=["xin0"])
                P.op("act", lambda e: e.activation(out=xs[:], in_=xin0[:], func=AF.Square, accum_out=small[:, 0:1]),
                     reads=["xin0"], writes=["xs", "sm_ss"], cost=1.0)
                P.op("dve", lambda e: e.tensor_scalar(out=small[:, 1:2], in0=small[:, 0:1], scalar1=1.0 / D, scalar2=EPS,
                                                      op0=ALU.mult, op1=ALU.add), reads=["sm_ss"], writes=["sm_var"])
                P.op("pool", lambda e: e.tensor_tensor(out=small[:, 2:3], in0=small[:, 1:2], in1=mhalf, op=ALU.pow),
                     reads=["sm_var", *SCK], writes=["sm_rstd"])
                P.op("dve", lambda e: e.tensor_scalar(out=xs[:], in0=xin0[:], scalar1=small[:, 2:3], scalar2=None, op0=ALU.mult),
                     reads=["xin0", "sm_rstd"], writes=["xs"], cost=0.8)
                yield
                yield from wait(*pre)

                def tr(e):
                    r = None
                    for kc in range(8):
                        r = e.transpose(out=psTX[:, kc * 128:(kc + 1) * 128], in_=xs[:, kc * 128:(kc + 1) * 128], identity=identB)
                    return r
                P.op("pe", tr, reads=["xs", "cmB"], writes=[KTX], cost=0.8)
                for kc in range(8):
                    if smp:
                        for sq in range(4):
                            P.op("act", (lambda e, kc=kc, sq=sq: e.activation(
                                out=hT_[:, kc, sq * 32:(sq + 1) * 32], in_=psTX[:, kc * 128 + sq * 32: kc * 128 + (sq + 1) * 32],
                                func=AF.Identity, scale=gm[:, kc, 1 + sq:2 + sq], bias=modfm[:, kc, 1 + sq:2 + sq])),
                                reads=[KTX, "gm", "modfm"], writes=[hTk])
                    elif mode == "fast" and kc % 2 == 1:
                        P.op("dve", (lambda e, kc=kc, j=j: e.tensor_scalar(
                            out=hT_[:, kc, j * 128:(j + 1) * 128], in0=psTX[:, kc * 128:(kc + 1) * 128],
                            scalar1=gm[:, kc, 0:1], scalar2=modfm[:, kc, 0:1], op0=ALU.mult, op1=ALU.add)),
                            reads=[KTX, "gm", "modfm"], writes=[hTk], cost=0.3)
                    else:
                        P.op("act", (lambda e, kc=kc, j=j: e.activation(
                            out=hT_[:, kc, j * 128:(j + 1) * 128], in_=psTX[:, kc * 128:(kc + 1) * 128],
                            func=AF.Identity, scale=gm[:, kc, 0:1], bias=modfm[:, kc, 0:1])),
                            reads=[KTX, "gm", "modfm"], writes=[hTk])
                yield
            if tag is not None:
                flags.add(tag)

        def fm_pair(col0, slot, TS, bs):
            hT_, hTk = bs["hT"], bs["hTk"]

            def mm(e):
                r = None
                for c in range(2):
                    for kc in range(8):
                        r = e.matmul(psF[slot][:, c * TS:(c + 1) * TS], lhsT=win[:, kc, col0 + c * 128: col0 + (c + 1) * 128],
                                     rhs=hT_[:, kc, 0:TS], start=(kc == 0), stop=(kc == 7))
                return r
            P.op("pe", mm, reads=[wkey(col0), wkey(col0 + 255), hTk], writes=[KF[slot]], cost=0.15 * TS / 16)

        def gen_Bx(mode, S, bs, need_c=True, pre=(), tag=None):
            smp, v, TS, nseg, Ls = modeinfo(mode, S)
            full = mode != "fast"
            yield from wait(*pre)
            for t0 in [0, 2, 4, 6, 8] + ([10] if (full or need_c) else []):
                slot = next_slot()
                fm_pair(C_XBC + t0 * 128, slot, TS, bs)
                xr, xrk = XR[slot], "XR%d" % slot
                xrv = xr[:, :, 0:nseg * (3 + Ls)].rearrange("p c (s l) -> p c s l", l=3 + Ls)
                P.op("act", (lambda e, xrv=xrv, t0=t0: e.copy(out=xrv[:, :, :, 0:3], in_=halo[:, t0:t0 + 2, 0:nseg, :])),
                     reads=["halo%d" % t0], writes=[xrk + "h"], cost=0.25)
                P.op("act", (lambda e, xrv=xrv, slot=slot: e.copy(
                    out=xrv[:, :, :, 3:3 + Ls], in_=psF[slot][:, 0:2 * TS].rearrange("p (c s l) -> p c s l", c=2, l=Ls))),
                    reads=[KF[slot]], writes=[xrk])
                P.op("act", (lambda e, xrv=xrv, t0=t0: e.copy(out=halo[:, t0:t0 + 2, 0:nseg, :], in_=xrv[:, :, :, Ls:Ls + 3])),
                     reads=[xrk], writes=["halo%d" % t0], cost=0.25)
                for c in range(2):
                    ti = t0 + c
                    ceng = "dve"
                    conv_i[0] += 1
                    ac, ack = acc[c], "acc%d" % c
                    acv = ac[:, 0:TS].rearrange("p (s l) -> p s l", l=Ls)
                    if full:
                        P.op("act", (lambda e, acv=acv, slot=slot, c=c, ti=ti: e.activation(
                            out=acv, in_=psF[slot][:, c * TS:(c + 1) * TS].rearrange("p (s l) -> p s l", l=Ls), func=AF.Identity,
                            scale=cw[:, ti, 3:4], bias=cb[:, ti:ti + 1])),
                            reads=[KF[slot], *SCK], writes=[ack], cost=0.45)
                    else:
                        P.op("dve", (lambda e, acv=acv, xrv=xrv, c=c, ti=ti: e.tensor_scalar(
                            out=acv, in0=xrv[:, c, :, 3:3 + Ls], scalar1=cw[:, ti, 3:4], scalar2=cb[:, ti:ti + 1], op0=ALU.mult, op1=ALU.add)),
                            reads=[xrk, xrk + "h", *SCK], writes=[ack])
                    for k in range(3):
                        P.op(ceng, (lambda e, acv=acv, xrv=xrv, c=c, ti=ti, k=k: e.scalar_tensor_tensor(
                            out=acv, in0=xrv[:, c, :, k:k + Ls], scalar=cw[:, ti, k:k + 1], in1=acv, op0=ALU.mult, op1=ALU.add)),
                            reads=[xrk, xrk + "h", ack, *SCK], writes=[ack])
                    if ti < 8:
                        dst, dk = bs["bufA"][:, ti, 0:TS], bs["bufAk"]
                    else:
                        dst, dk = bs["BCf"][:, ti - 8, 0:TS], bs["BCfk"]
                    P.op("act", (lambda e, ac=ac, dst=dst: e.activation(out=dst, in_=ac[:, 0:TS], func=AF.Silu)),
                         reads=[ack], writes=[dk], cls="S")
                yield
            if tag is not None:
                flags.add(tag)

        def gen_Bzug(mode, S, bs, pre=(), tagz=None, tagug=None):
            smp, v, TS, nseg, Ls = modeinfo(mode, S)
            yield from wait(*pre)
            for t0 in range(0, 8, 2):
                slot = next_slot()
                fm_pair(C_Z + t0 * 128, slot, TS, bs)
                P.op("act", (lambda e, slot=slot, t0=t0: e.activation(
                    out=bufB[:, t0:t0 + 2, 0:TS], in_=psF[slot][:, 0:2 * TS].rearrange("p (c t) -> p c t", c=2), func=AF.Silu)),
                    reads=[KF[slot]], writes=["bufB"], cls="S")
                yield
            flags.add(tagz)
            for t0 in range(0, 8, 2):
                slot = next_slot()
                fm_pair(C_G + t0 * 128, slot, TS, bs)
                P.op("act", (lambda e, slot=slot: e.activation(
                    out=sgt[:, :, 0:TS], in_=psF[slot][:, 0:2 * TS].rearrange("p (c t) -> p c t", c=2), func=AF.Silu)),
                    reads=[KF[slot]], writes=["sgt"], cls="S")
                slot = next_slot()
                fm_pair(C_U + t0 * 128, slot, TS, bs)
                P.op("dve", (lambda e, slot=slot, t0=t0: e.tensor_tensor(
                    out=bufC[:, t0:t0 + 2, 0:TS], in0=psF[slot][:, 0:2 * TS].rearrange("p (c t) -> p c t", c=2), in1=sgt[:, :, 0:TS],
                    op=ALU.mult)), reads=[KF[slot], "sgt"], writes=["bufC"])
                yield
            flags.add(tagug)

        def gen_C(mode, j, bs, ts, pre=(), pre_z=(), pre_state=(), pre_y=(), tag_state=None, tag=None):
            smp = mode == "sample"
            full = mode != "fast"
            v = 1 if smp else 0
            maskF, m1F = cmF[:, 1 + 2 * v, :], cmF[:, 2 + 2 * v, :]
            maskB, m1B = cmB[:, 1 + 2 * v, :], cmB[:, 2 + 2 * v, :]
            onesF = cmF[:, 5, :]
            hT_, hTk = bs["hT"], bs["hTk"]
            bufA_, bufAk, BCf_, BCfk = bs["bufA"], bs["bufAk"], bs["BCf"], bs["BCfk"]
            js = slice(j * 128, (j + 1) * 128)
            sm = small[:, 0:192]
            K = lambda n: "sm%d_%s" % (ts, n)
            LX_, XDD_, ZS_, Btm_, mCBT_ = LX[ts], XDD[ts], ZS[ts], Btm[ts], mCBT[ts]
            LXk, XDDk, ZSk, Btmk, mCBTk = "LX0", "XDD0", "ZS0", "Btm0", "mCBT0"
            dtr, dte, dt_, dA = sm[:, 16:32], sm[:, 32:48], sm[:, 48:64], sm[:, 64:80]
            ntot = 4 if smp else 1
            eae = sm[:, 80:80 + 32 + 16 * ntot]
            ea, dend = sm[:, 80:96], sm[:, 96:112]
            dtd = sm[:, 176:192]
            yield from wait(*pre)

            def mmdt(e):
                r = None
                for kc in range(8):
                    r = e.matmul(psMi[:, 0:16], lhsT=hT_[:, kc, js], rhs=win[:, kc, C_DT:C_DT + 16], start=(kc == 0), stop=(kc == 7))
                return r
            P.op("pe", mmdt, reads=[hTk, wkey(C_DT)], writes=[KMI])
            P.op("dve", lambda e: e.tensor_tensor(out=dtr, in0=psMi[:, 0:16], in1=dtb, op=ALU.add),
                 reads=[KMI, *SCK], writes=[K("dtr")])
            yield
            P.op("act", lambda e: e.activation(out=dte, in_=dtr, func=AF.Exp), reads=[K("dtr")], writes=[K("dte")], cost=1.0, cls="E")
            P.op("act", lambda e: e.activation(out=dt_, in_=dte, func=AF.Ln, bias=1.0), reads=[K("dte")], writes=[K("dt")], cost=1.0, cls="E")
            P.op("dve", lambda e: e.tensor_tensor(out=dA, in0=dt_, in1=Abc, op=ALU.mult),
                 reads=[K("dt"), *SCK], writes=[K("dA")])
            yield
            if smp:
                dma("sp", accT[:].rearrange("p a (q n) -> p (a q) n", n=128), segsel_d, "segsel", writes=["acc0", "acc1"])

            def mme(e):
                e.matmul(psMi[:, 16:32], lhsT=maskF, rhs=dA, start=True, stop=True)
                r = e.matmul(psMi[:, 32:48], lhsT=m1F, rhs=dA, start=True, stop=True)
                for q in range(ntot):
                    lt = segsel[:, q, :] if smp else onesF
                    r = e.matmul(psMi[:, 48 + 16 * q:64 + 16 * q], lhsT=lt, rhs=dA, start=True, stop=True)
                return r
            P.op("pe", mme, reads=[K("dA"), "cmF", "acc0", "acc1"] if smp else [K("dA"), "cmF"], writes=[KMI])
            P.op("act", lambda e: e.activation(out=eae, in_=psMi[:, 16:48 + 16 * ntot], func=AF.Exp),
                 reads=[KMI], writes=[K("eae")], cost=0.8, cls="E")
            P.op("dve", lambda e: e.tensor_tensor(out=dtd, in0=dt_, in1=dend, op=ALU.mult),
                 reads=[K("dt"), K("eae")], writes=[K("dtd")])
            yield

            def trx(e):
                r = None
                for i in range(8):
                    r = e.transpose(out=psTX[:, i * 128:(i + 1) * 128], in_=bufA_[:, i, js], identity=identB)
                return r
            P.op("pe", trx, reads=[bufAk, "cmB"], writes=[KTX], cost=0.8)
            psXv = psTX[:].rearrange("p (h q) -> p h q", q=64)
            if full:
                P.op("dve", lambda e: e.tensor_tensor(out=LX_[:].rearrange("p (h q) -> p h q", q=64), in0=psXv,
                                                      in1=dt_[:, :, None].to_broadcast([128, 16, 64]), op=ALU.mult),
                     reads=[KTX, K("dt")], writes=[LXk], cost=1.2)
            P.op("dve", lambda e: e.tensor_tensor(out=XDD_[:].rearrange("p (h q) -> p h q", q=64), in0=psXv,
                                                  in1=dtd[:, :, None].to_broadcast([128, 16, 64]), op=ALU.mult),
                 reads=[KTX, K("dtd")], writes=[XDDk], cost=1.2)
            yield
            psBv = psMi[:, 384:512].bitcast(BF16)

            def trb(e):
                r = None
                for g in range(2):
                    r = e.transpose(out=psBv[:, g * 128:(g + 1) * 128], in_=BCf_[:, g, js], identity=identB)
                return r
            P.op("pe", trb, reads=[BCfk, "cmB"], writes=[KMI])
            P.op("act", lambda e: e.copy(out=Btm_[:], in_=psBv), reads=[KMI], writes=[Btmk])
            yield

            if full:
                def mmcb(e):
                    r = None
                    for g in range(2):
                        r = e.matmul(psMi[:, 128 + g * 128:256 + g * 128], lhsT=BCf_[:, g, js], rhs=BCf_[:, 2 + g, js], start=True, stop=True)
                    return r
                P.op("pe", mmcb, reads=[BCfk], writes=[KMI])
                P.op("dve", lambda e: e.tensor_tensor(out=mCBT_[:], in0=psMi[:, 128:384].rearrange("p (g l) -> p g l", g=2),
                                                      in1=maskF[:, None, :].to_broadcast([128, 2, 128]), op=ALU.mult),
                     reads=[KMI, "cmF"], writes=[mCBTk])
                yield
                yield from wait(*pre_z)
                if smp:
                    for sq in range(4):
                        dma("sp", H[:], sssm_d[sq], "hl", writes=["H"])
                        P.op("act", lambda e: e.copy(out=Hbf[:], in_=H[:]), reads=["H"], writes=["Hbf"], cost=1.05)

                        def mmz(e, sq=sq):
                            r = None
                            for g in range(2):
                                r = e.matmul(psP[32 * sq:32 * sq + 32, g * 512:(g + 1) * 512], lhsT=BCf_[:, 2 + g, 32 * sq:32 * sq + 32],
                                             rhs=Hbf[:, g * 512:(g + 1) * 512], start=True, stop=True, tile_position=(0, 32 * sq))
                            return r
                        P.op("pe", mmz, reads=[BCfk, "Hbf"], writes=KP, cost=0.6)
                else:
                    def mmz(e):
                        r = None
                        for g in range(2):
                            r = e.matmul(psP[:, g * 512:(g + 1) * 512], lhsT=BCf_[:, 2 + g, js], rhs=Hbf[:, g * 512:(g + 1) * 512],
                                         start=True, stop=True)
                        return r
                    P.op("pe", mmz, reads=[BCfk, "Hbf"], writes=KP, cost=0.6)
                P.op("dve", lambda e: e.tensor_tensor(out=ZS_[:].rearrange("p (h q) -> p h q", q=64),
                                                      in0=psP[:].rearrange("p (h q) -> p h q", q=64),
                                                      in1=ea[:, :, None].to_broadcast([128, 16, 64]), op=ALU.mult),
                     reads=KP + [K("eae")], writes=[ZSk], cost=1.15)
                yield

            yield from wait(*pre_state)
            if smp:
                for sq in range(4):
                    rows = slice(32 * sq, 32 * sq + 32)

                    def mms(e, rows=rows, sq=sq):
                        r = None
                        for g in range(2):
                            r = e.matmul(psQ[:, g * 512:(g + 1) * 512], lhsT=Btm_[rows, g * 128:(g + 1) * 128],
                                         rhs=XDD_[rows, g * 512:(g + 1) * 512], start=True, stop=True, tile_position=(32 * sq, 0))
                        return r
                    P.op("pe", mms, reads=[Btmk, XDDk], writes=KQ, cost=0.6)
                    dma("sp", H[:], sssm_d[sq], "hl", writes=["H"])
                    dec = sm[:, 112 + 16 * sq:128 + 16 * sq]
                    P.op("dve", (lambda e, dec=dec: e.tensor_tensor(out=H[:].rearrange("p (h q) -> p h q", q=64),
                                                                    in0=H[:].rearrange("p (h q) -> p h q", q=64),
                                                                    in1=dec[:, :, None].to_broadcast([128, 16, 64]), op=ALU.mult)),
                         reads=["H", K("eae")], writes=["H"], cost=1.9)
                    P.op("dve", lambda e: e.tensor_tensor(out=H[:], in0=H[:], in1=psQ[:], op=ALU.add),
                         reads=["H"] + KQ, writes=["H"], cost=1.15)
                    dma("sp", ssms_o[sq], H[:], "hs", reads=["H"])
            else:
                def mms(e):
                    r = None
                    for g in range(2):
                        r = e.matmul(psQ[:, g * 512:(g + 1) * 512], lhsT=Btm_[:, g * 128:(g + 1) * 128],
                                     rhs=XDD_[:, g * 512:(g + 1) * 512], start=True, stop=True)
                    return r
                P.op("pe", mms, reads=[Btmk, XDDk], writes=KQ, cost=0.6)
                dec = sm[:, 112:128]
                P.op("dve", lambda e: e.tensor_tensor(out=H[:].rearrange("p (h q) -> p h q", q=64),
                                                      in0=H[:].rearrange("p (h q) -> p h q", q=64),
                                                      in1=dec[:, :, None].to_broadcast([128, 16, 64]), op=ALU.mult),
                     reads=["H", K("eae")], writes=["H"], cost=1.15)
                P.op("dve", lambda e: e.tensor_tensor(out=H[:], in0=H[:], in1=psQ[:], op=ALU.add),
                     reads=["H"] + KQ, writes=["H"], cost=1.15)
                if full:
                    P.op("act", lambda e: e.copy(out=Hbf[:], in_=H[:]), reads=["H"], writes=["Hbf"], cost=1.05)
            if tag_state is not None:
                flags.add(tag_state)
            yield

            if full:
                yield from wait(*pre_y)
                for hh in range(2):
                    P.op("dve", (lambda e, hh=hh: e.tensor_tensor(
                        out=M2[:], in0=dA[:, hh * 8:(hh + 1) * 8, None].to_broadcast([128, 8, 128]),
                        in1=maskB[:, None, :].to_broadcast([128, 8, 128]), op=ALU.mult)),
                        reads=[K("dA"), "cmB"], writes=["M2"], cost=1.15)
                    for q in range(2):
                        slot = next_slot()
                        P.op("pe", (lambda e, slot=slot, q=q: e.matmul(psF[slot][:, :], lhsT=m1B, rhs=M2[:, q * 4:(q + 1) * 4, :],
                                                                       start=True, stop=True)),
                             reads=["M2", "cmB"], writes=[KF[slot]])
                        P.op("act", (lambda e, slot=slot, q=q: e.activation(
                            out=EW[:, q * 4:(q + 1) * 4, :], in_=psF[slot][:, :].rearrange("p (h l) -> p h l", h=4), func=AF.Exp)),
                            reads=[KF[slot]], writes=["EW0"], cost=0.7, cls="E")
                    P.op("dve", (lambda e, hh=hh: e.tensor_tensor(
                        out=EW[:], in0=EW[:], in1=mCBT_[:, hh:hh + 1, :].to_broadcast([128, 8, 128]), op=ALU.mult)),
                        reads=["EW0", mCBTk], writes=["EW0"], cost=1.15)

                    def mmy(e, hh=hh):
                        r = None
                        for hl in range(8):
                            h = hh * 8 + hl
                            i, half = h // 2, h % 2
                            o_ = psQ[64 * half:64 * half + 64, i * 128:(i + 1) * 128]
                            e.matmul(o_, lhsT=LX_[:, h * 64:(h + 1) * 64], rhs=EW[:, hl, :], start=True, stop=False,
                                     tile_position=(0, 64 * half))
                            r = e.matmul(o_, lhsT=ZS_[:, h * 64:(h + 1) * 64], rhs=identB, start=False, stop=True,
                                         tile_position=(0, 64 * half))
                        return r
                    P.op("pe", mmy, reads=[LXk, ZSk, "EW0", "cmB"], writes=[KQ[hh]], cost=1.5)
                ydv = tmpo[:].rearrange("p (i l) -> p i l", l=128)
                for i in range(8):
                    P.op("dve", (lambda e, i=i: e.scalar_tensor_tensor(
                        out=ydv[:, i, :], in0=bufA_[:, i, js], scalar=Dp[:, i:i + 1], in1=psQ[:, i * 128:(i + 1) * 128],
                        op0=ALU.mult, op1=ALU.add)), reads=[bufAk, *SCK, KQ[i // 4]], writes=["tmpo"])
                P.op("dve", lambda e: e.tensor_tensor(out=bufB[:, :, js], in0=ydv, in1=bufB[:, :, js], op=ALU.mult),
                     reads=["tmpo", "bufB"], writes=["bufB"], cost=1.2)
                yield
            if tag is not None:
                flags.add(tag)

        def gen_D(mode, j, bs, ts, pre=(), tag_v=None, tag=None):
            smp = mode == "sample"
            hT_, hTk = bs["hT"], bs["hTk"]
            js = slice(j * 128, (j + 1) * 128)
            sm = small[:, 192:256]
            K = lambda n: "sm%d_%s" % (ts, n)
            yield from wait(*pre)

            def mmv(e):
                r = None
                for kc in range(8):
                    for cbk in range(2):
                        r = e.matmul(psP[:, cbk * 512:(cbk + 1) * 512], lhsT=hT_[:, kc, js],
                                     rhs=win[:, kc, C_V + cbk * 512:C_V + (cbk + 1) * 512], start=(kc == 0), stop=(kc == 7))
                return r
            P.op("pe", mmv, reads=[hTk, wkey(C_V), wkey(C_V + 512)], writes=KP, cost=4.8)
            if tag_v is not None:
                flags.add(tag_v)
            for cbk in range(2):
                P.op("dve", (lambda e, cbk=cbk: e.bn_stats(out=stats[:, cbk, :], in_=psP[:, cbk * 512:(cbk + 1) * 512])),
                     reads=[KP[cbk]], writes=["stats"])
            P.op("dve", lambda e: e.bn_aggr(out=sm[:, 4:6], in_=stats[:]), reads=["stats"], writes=[K("mv")])
            P.op("dve", lambda e: e.tensor_scalar(out=sm[:, 6:7], in0=sm[:, 5:6], scalar1=EPS, scalar2=None, op0=ALU.add),
                 reads=[K("mv")], writes=[K("v2")])
            P.op("pool", lambda e: e.tensor_tensor(out=sm[:, 7:8], in0=sm[:, 6:7], in1=mhalf, op=ALU.pow),
                 reads=[K("v2"), *SCK], writes=[K("vr")])
            P.op("dve", lambda e: e.scalar_tensor_tensor(out=sm[:, 8:9], in0=sm[:, 4:5], scalar=-1.0, in1=sm[:, 7:8],
                                                         op0=ALU.mult, op1=ALU.mult), reads=[K("mv"), K("vr")], writes=[K("nmr")])
            if smp:
                P.op("act", lambda e: e.activation(out=tmpo[:], in_=psP[:], func=AF.Identity, scale=sm[:, 7:8], bias=sm[:, 8:9]),
                     reads=KP + [K("vr"), K("nmr")], writes=["tmpo"])
                dma("sp", xin0[:], gvB_d, "gvl2", writes=["xin0"])
                dma("sp", res[:], bvB_d, "bvl2", writes=["res"])
                P.op("dve", lambda e: e.tensor_tensor(out=tmpo[:], in0=tmpo[:], in1=xin0[:], op=ALU.mult),
                     reads=["tmpo", "xin0"], writes=["tmpo"])
                P.op("dve", lambda e: e.tensor_tensor(out=tmpo[:], in0=tmpo[:], in1=res[:], op=ALU.add),
                     reads=["tmpo", "res"], writes=["tmpo"])
                dma("sp", vns_o, tmpo[:], "vno", reads=["tmpo"])
            P.op("act", lambda e: e.activation(out=vhat[:], in_=psP[:], func=AF.Identity, scale=sm[:, 7:8], bias=sm[:, 8:9]),
                 reads=KP + [K("vr"), K("nmr")], writes=["vhat"], cost=1.05)
            P.op("dve", lambda e: e.tensor_tensor(out=vhat[:], in0=vhat[:], in1=gvB[:], op=ALU.mult),
                 reads=["vhat", "gvB"], writes=["vhat"], cost=1.0)

            def mmm(e):
                r = None
                for g in range(8):
                    o_ = psQ[:, g * 128:(g + 1) * 128]
                    e.matmul(o_, lhsT=vhat[:, g * 128:(g + 1) * 128], rhs=WsT[:, g, :], start=True, stop=False)
                    r = e.matmul(o_, lhsT=identB, rhs=Rhl[:, 0, g, :], start=False, stop=True)
                return r
            P.op("pe", mmm, reads=["vhat", "WsT0", "Rhl", "cmB"], writes=KQ, cost=1.2)
            P.op("dve", lambda e: e.tensor_tensor(out=bufC[:, :, js], in0=psQ[:].rearrange("p (g t) -> p g t", g=8),
                                                  in1=bufC[:, :, js], op=ALU.mult),
                 reads=KQ + ["bufC"], writes=["bufC"], cost=1.2)
            yield
            if tag is not None:
                flags.add(tag)

        def gen_E(mode, j, xsrc, ydst, ts, tag=None):
            js = slice(j * 128, (j + 1) * 128)
            sm = small[:, 192:256]
            K = lambda n: "sm%d_%s" % (ts, n)

            def mmo(e):
                r = None
                for kc in range(16):
                    src = bufB if kc < 8 else bufC
                    for cbk in range(2):
                        r = e.matmul(psP[:, cbk * 512:(cbk + 1) * 512], lhsT=src[:, kc % 8, js],
                                     rhs=wout[:, kc, cbk * 512:(cbk + 1) * 512], start=(kc == 0), stop=(kc == 15))
                return r
            P.op("pe", mmo, reads=["bufB", "bufC", "wout0", "wout1", "wout2", "wout3"], writes=KP, cost=9.3)
            dma("sp", res[:], xsrc[j * 128:(j + 1) * 128, :], "rl", writes=["res"])
            P.op("act", lambda e: e.activation(out=tmpo[:], in_=psP[:], func=AF.Square, accum_out=sm[:, 9:10]),
                 reads=KP, writes=["tmpo", K("ss2")], cost=1.0)
            P.op("dve", lambda e: e.tensor_scalar(out=sm[:, 10:11], in0=sm[:, 9:10], scalar1=1.0 / D, scalar2=EPS,
                                                  op0=ALU.mult, op1=ALU.add), reads=[K("ss2")], writes=[K("var2")])
            P.op("pool", lambda e: e.tensor_tensor(out=sm[:, 11:12], in0=sm[:, 10:11], in1=mhalf, op=ALU.pow),
                 reads=[K("var2"), *SCK], writes=[K("rstd2")])
            P.op("dve", lambda e: e.scalar_tensor_tensor(out=tmpo[:], in0=psP[:], scalar=sm[:, 11:12], in1=GG[:],
                                                         op0=ALU.mult, op1=ALU.mult), reads=KP + [K("rstd2"), "GG"], writes=["tmpo"], cost=1.2)
            P.op("dve", lambda e: e.tensor_tensor(out=res[:], in0=res[:], in1=tmpo[:], op=ALU.add),
                 reads=["res", "tmpo"], writes=["res"], cost=1.15)
            dma("sp", ydst[j * 128:(j + 1) * 128, :], res[:], "yo", reads=["res"])
            yield
            if tag is not None:
                flags.add(tag)

        def chain(*gens):
            for g in gens:
                yield from g

        def rr(*gens):
            gens = list(gens)
            guard = 0
            while gens:
                progressed = False
                n0 = len(P.ops)
                for g in list(gens):
                    try:
                        next(g)
                    except StopIteration:
                        gens.remove(g)
                        progressed = True
                if len(P.ops) != n0:
                    progressed = True
                guard = 0 if progressed else guard + 1
                assert guard < 50, "scheduling deadlock"

        nf = n_fast_st
        bsets = [BS0, BS1]

        def fast_ab():
            for i in range(nf):
                bs = bsets[i % 2]
                pre = [("fc", i - 2)] if i >= 2 else []
                yield from gen_A("fast", xp[i * 256:(i + 1) * 256, :], 2, bs, pre=pre)
                yield from gen_Bx("fast", 2, bs, need_c=(i == nf - 1), pre=pre, tag=("fb", i))

        def fast_c():
            for i in range(nf):
                bs = bsets[i % 2]
                for j in range(2):
                    yield from gen_C("fast", j, bs, 0, pre=[("fb", i)])
                flags.add(("fc", i))

        rr(fast_ab(), fast_c())
        P.op("dve", lambda e: e.tensor_scalar(out=H[:], in0=H[:], scalar1=flg, scalar2=None, op0=ALU.mult),
             reads=["H", *SCK], writes=["H"])
        P.op("act", lambda e: e.copy(out=Hbf[:], in_=H[:]), reads=["H"], writes=["Hbf"], cost=1.05)
        P.op("dve", lambda e: e.tensor_scalar(out=halo[:], in0=halo[:], scalar1=flg, scalar2=None, op0=ALU.mult),
             reads=HK + [*SCK], writes=HK)

        nm = n_full_st

        def full_ab():
            for i in range(nm):
                bs = bsets[i % 2]
                xsrc = xc[i * 256:(i + 1) * 256, :]
                pre2 = [("dv", i - 2, 0), ("dv", i - 2, 1), ("cdone", i - 2, 0), ("cdone", i - 2, 1)] if i >= 2 else []
                yield from gen_A("full", xsrc, 2, bs, pre=pre2)
                yield from gen_Bx("full", 2, bs, pre=pre2, tag=("bx", i))
                preZ = [("edone", i - 1, 0), ("edone", i - 1, 1)] if i > 0 else []
                yield from gen_Bzug("full", 2, bs, pre=preZ, tagz=("bz", i), tagug=("bug", i))

        def full_c():
            for i in range(nm):
                bs = bsets[i % 2]
                for j in range(2):
                    yield from gen_C("full", j, bs, 0, pre=[("bx", i)], pre_y=[("bz", i)], tag=("cdone", i, j))

        def full_de():
            for i in range(nm):
                bs = bsets[i % 2]
                xsrc = xc[i * 256:(i + 1) * 256, :]
                ydst = yc[i * 256:(i + 1) * 256, :]
                for j in range(2):
                    yield from gen_D("full", j, bs, 1, pre=[("bug", i)], tag_v=("dv", i, j))
                    yield from wait(("cdone", i, j))
                    yield from gen_E("full", j, xsrc, ydst, 1, tag=("edone", i, j))

        rr(full_ab(), full_c(), full_de())
        dma("sp", ssmp_o, H[:], "hpo", reads=["H"])
        dma("sp", convp_o, halo[:, :, 0, :], "cpo", reads=HK)

        dma("sp", halo[:], sconv_d, "hls", reads=HK, writes=HK)
        dma("sp", tmpo[0:5, :], gr5_d, "g5l", reads=["gr5d"], writes=["tmpo"])
        build_GG(1)
        dma("sp", WsT[:], wss_d, "wssl", reads=["wssd"], writes=["WsT0"])
        dma("sp", Rhl[:], rhs_d, "rhsl", reads=["rhsd"], writes=["Rhl"])
        flags.add("s_go")
        rr(chain(gen_A("sample", xsm, 1, BS0), gen_Bx("sample", 1, BS0), gen_Bzug("sample", 1, BS0, tagz="s_bz", tagug="s_bug"),
                 gen_C("sample", 0, BS0, 0), gen_D("sample", 0, BS0, 0), gen_E("sample", 0, xsm, ysm, 0)))
        dma("sp", convs_o, halo[:], "cso", reads=HK)
        P.finalize()
    return nc


def _consts():
    t = np.arange(128)
    ident = np.eye(128, dtype=np.float32)
    mask_p = (t[:, None] <= t[None, :]).astype(np.float32)
    m1_p = (t[:, None] > t[None, :]).astype(np.float32)
    same = (t[:, None] // 32 == t[None, :] // 32)
    mask_s = (same & (t[:, None] <= t[None, :])).astype(np.float32)
    m1_s = (same & (t[:, None] > t[None, :])).astype(np.float32)
    ones = np.ones((128, 128), np.float32)
    cmask = np.stack([ident, mask_p, m1_p, mask_s, m1_s, ones])
    segsel = np.zeros((128, 4, 128), np.float32)
    for q in range(4):
        segsel[32 * q:32 * q + 32, q, :] = 1.0
    sel5 = np.zeros((2, 5, 128), np.float32)
    sel5[0, 0, :] = 1.0
    for q in range(4):
        sel5[1, 1 + q, 32 * q:32 * q + 32] = 1.0
    return cmask, segsel, sel5


def kernel(x_prompt, x_sample, state_conv, state_ssm, c_prompt, c_sample, w_ada, b_ada, g_pre,
           g_post, w_in, conv_w, conv_b, dt_bias, a_log, d_skip, g_v, beta_v, w_s, b_s, w_out):
    f = lambda a: np.ascontiguousarray(np.asarray(a, dtype=np.float32))
    x_prompt, x_sample = f(x_prompt), f(x_sample)
    B, SEQ, _ = x_prompt.shape
    half = SEQ // 2
    n_st = half // 256
    NB = x_sample.shape[0] // 8
    assert NB == 4 and B == 4 and half % 256 == 0
    nc = build(n_st, n_st)
    cmask, segsel, sel5 = _consts()
    w_ada0, b_ada0, w_in0, w_out0 = f(w_ada)[0], f(b_ada)[0], f(w_in)[0], f(w_out)[0]
    rep = lambda vec, n=128: np.ascontiguousarray(np.broadcast_to(np.asarray(vec, np.float32)[None, :], (n, len(vec))))
    ws0 = f(w_s)[0]
    wsT_p = np.ascontiguousarray(ws0.transpose(2, 0, 1))
    wsT_s = np.zeros((128, 8, 128), np.float32)
    blk = ws0[:, :32, :32].transpose(2, 0, 1)
    for q in range(4):
        wsT_s[32 * q:32 * q + 32, :, 32 * q:32 * q + 32] = blk
    bs0 = f(b_s)[0]
    bs_p = bs0[None]
    bs_s = np.tile(bs0[:, :32], (1, 4))[None]
    shared = dict(
        w_ada=w_ada0, bada_fm=np.ascontiguousarray(b_ada0[:2048].reshape(16, 128).T), bada_g=rep(b_ada0[2048:], 5),
        gpre_fm=np.ascontiguousarray(f(g_pre)[0].reshape(8, 128).T), gpost5=rep(f(g_post)[0], 5),
        w_in=w_in0, w_out=w_out0,
        cw=np.ascontiguousarray(f(conv_w)[0].reshape(4, 12, 128).transpose(2, 1, 0)),
        cb=np.ascontiguousarray(f(conv_b)[0].reshape(12, 128).T),
        dtb=rep(f(dt_bias)[0]), alog=rep(f(a_log)[0]),
        Dp=np.ascontiguousarray(np.repeat(f(d_skip)[0].reshape(8, 2), 64, axis=1).T),
        gvB=rep(f(g_v)[0]), bvB=rep(f(beta_v)[0]),
        brow=np.ascontiguousarray(np.stack([f(beta_v)[0], np.ones(D, np.float32)])),
        wsT=np.ascontiguousarray(np.stack([wsT_p, wsT_s])), bsrow=np.ascontiguousarray(np.stack([bs_p, bs_s])),
        cmask=cmask, segsel=segsel, sel5=sel5,
    )
    sc0, ss0 = f(state_conv)[0], f(state_ssm)[0]
    in_maps = []
    for core in range(8):
        s, hb = core // 2, core % 2
        sq = slice(4 * core, 4 * core + 4)
        m = dict(shared)
        m["xp"] = x_prompt[s, 0:half]
        m["xc"] = x_prompt[s, hb * half:(hb + 1) * half]
        m["xsm"] = np.ascontiguousarray(x_sample[sq].reshape(128, D))
        m["flag"] = np.full((128, 1), float(hb), np.float32)
        crow = np.concatenate([f(c_prompt)[s:s + 1], f(c_sample)[sq]], 0)
        m["cT"] = np.ascontiguousarray(crow.reshape(5, 8, 128).transpose(2, 1, 0))
        m["sconv"] = np.ascontiguousarray(sc0[sq].reshape(4, 3, 12, 128).transpose(3, 2, 0, 1))
        m["sssm"] = np.ascontiguousarray(ss0[sq].reshape(4, D, 128).transpose(0, 2, 1))
        in_maps.append(m)
    out = run_bass_kernel_spmd(nc, in_maps, core_ids=list(range(8)))
    R = out.results
    y_prompt = np.empty((B, SEQ, D), np.float32)
    y_sample = np.empty_like(x_sample)
    conv_prompt = np.empty((1, B, 3, 1536), np.float32)
    ssm_prompt = np.empty((1, B, 16, 64, 128), np.float32)
    conv_sample = np.empty((1, 32, 3, 1536), np.float32)
    ssm_sample = np.empty((1, 32, 16, 64, 128), np.float32)
    vn = np.empty((1, 32, 32, D), np.float32)
    for core in range(8):
        s, hb = core // 2, core % 2
        r = R[core]
        y_prompt[s, hb * half:(hb + 1) * half] = r["yc"]
        y_sample[4 * core:4 * core + 4] = r["ysm"].reshape(4, 32, D)
        if hb == 1:
            conv_prompt[0, s] = r["convp"].transpose(2, 1, 0).reshape(3, 1536)
            ssm_prompt[0, s] = r["ssmp"].T.reshape(16, 64, 128)
        conv_sample[0, 4 * core:4 * core + 4] = r["convs"].transpose(2, 3, 1, 0).reshape(4, 3, 1536)
        ssm_sample[0, 4 * core:4 * core + 4] = r["ssms"].transpose(0, 2, 1).reshape(4, 16, 64, 128)
        vn[0, 4 * core:4 * core + 4] = r["vns"].reshape(4, 32, D)
    return (y_prompt, y_sample, conv_prompt, ssm_prompt, conv_sample, ssm_sample, vn)
```

```python
import numpy as np
from contextlib import ExitStack
import concourse.bass as bass
import concourse.mybir as mybir
from concourse.bass_utils import run_bass_kernel_spmd

F32 = mybir.dt.float32
BF16 = mybir.dt.bfloat16
AF = mybir.ActivationFunctionType
ALU = mybir.AluOpType

D = 1024
NCOL = 5648
C_Z, C_XBC, C_DT, C_U, C_V, C_G = 0, 1024, 2560, 2576, 3600, 4624
EPS = 1e-6


class _Op:
    __slots__ = ("eng", "fn", "reads", "writes", "dma", "idx", "waits", "mile", "mval", "dcount", "cost", "cls")


_DEF_COST = {"pe": 0.3, "act": 0.5, "dve": 0.5, "pool": 1.0, "sp": 0.1}


class Prog:
    ENGS = ("pe", "act", "dve", "pool", "sp")

    def __init__(self, nc):
        self.nc = nc
        self.ops = []

    def op(self, eng, fn, reads=(), writes=(), dma=None, cost=None, cls=None):
        o = _Op()
        o.cls = cls
        o.eng = eng
        o.fn = fn
        r, w = [], list(writes)
        for k in reads:
            if k.startswith("ps:"):
                if k not in w:
                    w.append(k)
            else:
                r.append(k)
        o.reads = tuple(r)
        o.writes = tuple(w)
        o.dma = dma
        o.idx = len(self.ops)
        o.waits = []
        o.mile = False
        o.mval = 0
        o.dcount = 0
        o.cost = cost if cost is not None else (4.0 if dma is not None else _DEF_COST[eng])
        self.ops.append(o)
        return o

    def finalize(self, reorder=True):
        ops = self.ops
        n = len(ops)
        last_w, readers = {}, {}
        preds = [None] * n
        for o in ops:
            i = o.idx
            d = {}
            for k in o.reads:
                j = last_w.get(k)
                if j is not None:
                    d[j] = True
            for k in o.writes:
                j = last_w.get(k)
                if j is not None:
                    d[j] = True
                for j in readers.get(k, ()):
                    if j not in d:
                        d[j] = False
            d.pop(i, None)
            for k in o.reads:
                readers.setdefault(k, []).append(i)
            for k in o.writes:
                last_w[k] = i
                readers[k] = []
            preds[i] = d
        order = list(range(n))
        if reorder:
            order = self._schedule(preds)
        pos = [0] * n
        for p, i in enumerate(order):
            pos[i] = p
        for i in range(n):
            for j in preds[i]:
                assert pos[j] < pos[i], "schedule violates a dependency"
        dma_counts = {}
        for i in order:
            o = ops[i]
            if o.dma is not None:
                c = dma_counts.get(o.dma[0], 0) + 16 * o.dma[1]
                dma_counts[o.dma[0]] = c
                o.dcount = c
        self.dma_counts = dma_counts
        need = [None] * n
        for i in order:
            o = ops[i]
            best = {}
            for j, raw in preds[i].items():
                pj = ops[j]
                if pj.dma is not None:
                    key = ("dma", pj.dma[0])
                    best[key] = max(best.get(key, 0), pj.dcount)
                    continue
                if pj.eng == o.eng and o.eng in ("pe", "sp"):
                    continue
                key = ("eng", pj.eng)
                if key not in best or pos[best[key]] < pos[j]:
                    best[key] = j
            need[i] = best
            for key, j in best.items():
                if key[0] == "eng":
                    ops[j].mile = True
        cnt = {e: 0 for e in self.ENGS}
        for i in order:
            o = ops[i]
            if o.mile:
                cnt[o.eng] += 1
                o.mval = cnt[o.eng]
        seen = {e: {} for e in self.ENGS}
        for i in order:
            o = ops[i]
            ws = []
            for key, v in need[i].items():
                val = ops[v].mval if key[0] == "eng" else v
                if seen[o.eng].get(key, 0) >= val:
                    continue
                seen[o.eng][key] = val
                ws.append((key, val))
            o.waits = ws
        self.order = order
        self._emit()

    def _schedule(self, preds, delta=0.3, WIN=24, seed=0):
        import heapq
        ops = self.ops
        n = len(ops)
        succs = [[] for _ in range(n)]
        npred = [0] * n
        for i in range(n):
            npred[i] = len(preds[i])
            for j in preds[i]:
                succs[j].append(i)
        bl = [0.0] * n
        for i in range(n - 1, -1, -1):
            m = 0.0
            for s_ in succs[i]:
                if bl[s_] > m:
                    m = bl[s_]
            bl[i] = ops[i].cost + m
        if seed:
            rs = np.random.RandomState(seed)
            jit = rs.uniform(0.0, 0.6, size=n)
        else:
            jit = np.zeros(n)
        finish = [0.0] * n
        ready_t = [0.0] * n
        free = {e: 0.0 for e in self.ENGS}
        cand = {e: [] for e in self.ENGS}
        for i in range(n):
            if npred[i] == 0:
                heapq.heappush(cand[ops[i].eng], (0.0, -bl[i], i))
        order = []
        starts = [0.0] * n
        self.sim_starts = starts
        cur_cls = [None]
        while len(order) < n:
            bestc = None
            for e in self.ENGS:
                h = cand[e]
                if not h:
                    continue
                fe = free[e]
                top = heapq.nsmallest(WIN, h)
                ests = []
                for (rt, nb, i) in top:
                    est = rt if rt > fe else fe
                    if e == "act" and ops[i].cls is not None and cur_cls[0] is not None and ops[i].cls != cur_cls[0]:
                        est += 1.3
                    ests.append((est + jit[i], nb, i, rt))
                emin = min(x[0] for x in ests)
                pick = None
                for (est, nb, i, rt) in ests:
                    if est <= emin + delta:
                        key = (nb, est, i)
                        if pick is None or key < pick[2]:
                            pick = ((est - jit[i], nb, i), (rt, nb, i), key)
                if bestc is None or pick[0] < bestc[0]:
                    bestc = (pick[0], e, pick[1])
            (est, nb, i), e, item = bestc
            cand[e].remove(item)
            heapq.heapify(cand[e])
            o = ops[i]
            if e == "act" and o.cls is not None:
                cur_cls[0] = o.cls
            if o.dma is not None:
                free[e] = est + (1.0 if e == "pool" else 0.1)
                finish[i] = est + o.cost
            else:
                free[e] = est + o.cost
                finish[i] = est + o.cost
            order.append(i)
            starts[i] = est
            for s_ in succs[i]:
                lat = 0.0 if (ops[s_].eng == e and o.dma is None) else 0.15
                t = finish[i] + lat
                if t > ready_t[s_]:
                    ready_t[s_] = t
                npred[s_] -= 1
                if npred[s_] == 0:
                    heapq.heappush(cand[ops[s_].eng], (ready_t[s_], -bl[s_], s_))
        self.sim_makespan = max(finish) if n else 0.0
        return order

    def _emit(self):
        nc = self.nc
        with ExitStack() as st:
            esem = {e: st.enter_context(nc.semaphore("s_" + e)) for e in ("pe", "act", "dve", "pool")}
            dsem = {n: st.enter_context(nc.semaphore("d_" + n)) for n in self.dma_counts}
            block = st.enter_context(nc.Block())
            by_eng = {e: [self.ops[i] for i in self.order if self.ops[i].eng == e] for e in self.ENGS}

            def run(engname, engobj):
                for o in by_eng[engname]:
                    for key, val in o.waits:
                        s = esem[key[1]] if key[0] == "eng" else dsem[key[1]]
                        engobj.wait_ge(s, val)
                    res = o.fn(engobj)
                    if o.dma is not None:
                        if not isinstance(res, (list, tuple)):
                            res = [res]
                        assert len(res) == o.dma[1]
                        for r in res:
                            r.then_inc(dsem[o.dma[0]], 16)
                    elif o.mile:
                        if isinstance(res, (list, tuple)):
                            res = res[-1]
                        res.then_inc(esem[o.eng], 1)
                if engname == "sp":
                    for n, c in self.dma_counts.items():
                        engobj.wait_ge(dsem[n], c)

            block.tensor(lambda e: run("pe", e))
            block.scalar(lambda e: run("act", e))
            block.vector(lambda e: run("dve", e))
            block.gpsimd(lambda e: run("pool", e))
            block.sync(lambda e: run("sp", e))


IN_SPECS = None


def build(n_fast_st, n_full_st):
    nc = bass.Bass("TRN2", target_bir_lowering=False)
    NFT, NMT = n_fast_st * 256, n_full_st * 256

    def din(name, shape, dt=F32):
        return nc.dram_tensor(name, list(shape), dt, kind="ExternalInput").ap()

    def dout(name, shape, dt=F32):
        return nc.dram_tensor(name, list(shape), dt, kind="ExternalOutput").ap()

    xp = din("xp", [NFT, D]); xc = din("xc", [NMT, D]); xsm = din("xsm", [128, D])
    flag = din("flag", [128, 1]); cT = din("cT", [128, 8, 5])
    w_ada = din("w_ada", [D, 3 * D]); bada_fm = din("bada_fm", [128, 16]); bada_g = din("bada_g", [5, D])
    gpre_fm = din("gpre_fm", [128, 8]); gpost5 = din("gpost5", [5, D])
    w_in = din("w_in", [D, NCOL]); w_out = din("w_out", [2 * D, D])
    cw_d = din("cw", [128, 12, 4]); cb_d = din("cb", [128, 12])
    dtb_d = din("dtb", [128, 16]); alog_d = din("alog", [128, 16]); Dp_d = din("Dp", [128, 8])
    gvB_d = din("gvB", [128, D]); bvB_d = din("bvB", [128, D]); brow_d = din("brow", [2, D])
    wsT_d = din("wsT", [2, 128, 8, 128]); bs_d = din("bsrow", [2, 1, 8, 128])
    sconv_d = din("sconv", [128, 12, 4, 3]); sssm_d = din("sssm", [4, 128, D])
    cmask_d = din("cmask", [6, 128, 128])
    segsel_d = din("segsel", [128, 4, 128]); sel5_d = din("sel5", [2, 5, 128])

    yc = dout("yc", [NMT, D]); ysm = dout("ysm", [128, D])
    convp_o = dout("convp", [128, 12, 3]); ssmp_o = dout("ssmp", [128, D])
    convs_o = dout("convs", [128, 12, 4, 3]); ssms_o = dout("ssms", [4, 128, D]); vns_o = dout("vns", [128, D])

    P = Prog(nc)
    SCK = ["sc%d" % i for i in range(8)] + ["scm"]
    with ExitStack() as st:
        def sb(name, shape, dt):
            return st.enter_context(nc.sbuf_tensor(name, list(shape), dt))

        def ps(name, shape, dt=F32):
            return st.enter_context(nc.psum_tensor(name, list(shape), dt))

        win = sb("win", [128, 8, NCOL], BF16)
        wout = sb("wout", [128, 16, D], BF16)
        xin0 = sb("xin0", [128, D], F32)
        res = sb("res", [128, D], F32)
        xs = sb("xs", [128, D], BF16)
        hT = sb("hT", [128, 8, 256], BF16)
        XR = [sb("XR0", [128, 2, 259], F32), sb("XR1", [128, 2, 259], F32)]
        accT = sb("acc", [128, 2, 256], F32)
        acc = [accT[:, 0, :], accT[:, 1, :]]
        bufA = sb("bufA", [128, 8, 256], BF16)
        BCf = sb("BCf", [128, 4, 256], BF16)
        bufB = sb("bufB", [128, 8, 256], BF16)
        bufC = sb("bufC", [128, 8, 256], BF16)
        sgt = sb("sgt", [128, 2, 256], BF16)
        M2 = sb("M2", [128, 8, 128], BF16)
        EW = sb("EW0", [128, 8, 128], BF16)
        mCBT = [sb("mCBT0", [128, 2, 128], BF16)] * 2
        LX = [sb("LX0", [128, D], BF16)] * 2
        XDD = [sb("XDD0", [128, D], BF16)] * 2
        ZS = [sb("ZS0", [128, D], BF16)] * 2
        Btm = [sb("Btm0", [128, 256], BF16)] * 2
        hT2 = sb("hT2", [128, 8, 256], BF16)
        bufA2 = sb("bufA2", [128, 8, 256], BF16)
        BCf2 = sb("BCf2", [128, 4, 256], BF16)
        H = sb("H", [128, D], F32)
        Hbf = sb("Hbf", [128, D], BF16)
        vhat = sb("vhat", [128, D], BF16)
        gvB = sb("gvBb", [128, D], BF16)
        tmpo = sb("tmpo", [128, D], F32)
        GG = sb("GG", [128, D], F32)
        WsT = sb("WsT0", [128, 8, 128], BF16)
        Rhl = sb("Rhl", [128, 1, 8, 128], BF16)
        cmF = sb("cmF", [128, 6, 128], F32)
        cmB = sb("cmB", [128, 6, 128], BF16)
        sel5 = sb("sel5_s", [5, 2, 128], F32)
        scT = sb("scT", [128, 8, 5], F32)
        modfm = sb("modfm", [128, 16, 5], F32)
        gm = sb("gm", [128, 8, 5], F32)
        smallc = sb("smallc", [128, 128], F32)
        halo = sb("halo", [128, 12, 4, 3], F32)
        small = sb("small", [128, 256], F32)
        stats = sb("stats", [128, 2, 6], F32)
        segsel = accT[:].rearrange("p a (q n) -> p (a q) n", n=128)
        rowsA = H
        rowsB = res[:].rearrange("p (g n) -> p g n", n=128)
        gr5_d = nc.dram_tensor("gr5_scr", [5, D], F32).ap()
        wss_d = nc.dram_tensor("wss_scr", [128, 8, 128], BF16).ap()
        rhs_d = nc.dram_tensor("rhs_scr", [128, 1, 8, 128], BF16).ap()

        cw = smallc[:, 0:48].rearrange("p (i k) -> p i k", k=4)
        cb = smallc[:, 48:60]
        dtb = smallc[:, 60:76]
        Abc = smallc[:, 76:92]
        Dp = smallc[:, 92:100]
        gpre = smallc[:, 100:108]
        badaf = smallc[:, 108:124]
        mhalf = smallc[:, 124:125]
        flg = smallc[:, 125:126]

        psF = [ps("psF0", [128, 512]), ps("psF1", [128, 512])]
        psTX = ps("psTX", [128, D], BF16)
        psMi = ps("psMi", [128, 512])
        psP = ps("psP", [128, D])
        psQ = ps("psQ", [128, D])
        KF = ["ps:0", "ps:1"]
        KTX, KMI = "ps:2", "ps:3"
        KP, KQ = ["ps:4", "ps:5"], ["ps:6", "ps:7"]

        identB = cmB[:, 0, :]

        def dma(eng, out, in_, key, reads=(), writes=(), n=1):
            P.op(eng, (lambda e: e.dma_start(out=out, in_=in_)), reads=reads, writes=writes, dma=(key, n))

        HK = ["halo%d" % t for t in range(0, 12, 2)]
        dma("sp", cmF[:], cmask_d.rearrange("c p n -> p c n"), "cmF", writes=["cmF"])
        dma("sp", scT[:], cT, "scT", writes=["scT"])
        dma("sp", smallc[:, 0:48], cw_d.rearrange("p i k -> p (i k)"), "sc0", writes=["sc0"])
        dma("sp", smallc[:, 48:60], cb_d, "sc1", writes=["sc1"])
        dma("sp", smallc[:, 60:76], dtb_d, "sc2", writes=["sc2"])
        dma("sp", smallc[:, 76:92], alog_d, "sc3", writes=["sc3"])
        dma("sp", smallc[:, 92:100], Dp_d, "sc4", writes=["sc4"])
        dma("sp", smallc[:, 100:108], gpre_fm, "sc5", writes=["sc5"])
        dma("sp", smallc[:, 108:124], bada_fm, "sc6", writes=["sc6"])
        dma("sp", smallc[:, 125:126], flag, "sc7", writes=["sc7"])
        dma("sp", sel5[:], sel5_d.rearrange("v j n -> j v n"), "sel5", writes=["sel5"])

        P.op("pool", lambda e: e.memset(mhalf, -0.5), writes=["scm"])
        P.op("dve", lambda e: e.tensor_copy(out=cmB[:], in_=cmF[:]), reads=["cmF"], writes=["cmB"])
        P.op("act", lambda e: e.activation(out=scT[:], in_=scT[:], func=AF.Silu), reads=["scT"], writes=["scT"], cls="S")
        P.op("act", lambda e: e.activation(out=Abc, in_=Abc, func=AF.Exp), reads=["sc3"], writes=["sc3"], cls="E")
        P.op("dve", lambda e: e.tensor_scalar(out=Abc, in0=Abc, scalar1=-1.0, scalar2=None, op0=ALU.mult),
             reads=["sc3"], writes=["sc3"])

        pieces = [(C_XBC, C_XBC + 512), (C_XBC + 512, C_XBC + 1024), (C_XBC + 1024, C_DT + 16),
                  (C_Z, C_Z + 512), (C_Z + 512, C_Z + 1024),
                  (C_G, C_G + 512), (C_U, C_U + 512), (C_G + 512, C_G + 1024), (C_U + 512, C_U + 1024),
                  (C_V, C_V + 512), (C_V + 512, C_V + 1024)]
        w_in_v = w_in.rearrange("(k p) n -> p k n", p=128)
        for pi, (c0, c1) in enumerate(pieces[:3]):
            dma("pool", win[:, :, c0:c1], w_in_v[:, :, c0:c1], "win%d" % pi, writes=["win%d" % pi])

        def wkey(col):
            for pi, (c0, c1) in enumerate(pieces):
                if c0 <= col < c1:
                    return "win%d" % pi
            raise ValueError(col)

        wst = wout[:].bitcast(F32)
        stage = [wst[:, 0:8, :], wst[:, 8:16, :]]
        skeys = [["wout0", "wout1"], ["wout2", "wout3"]]
        gate5 = H[0:5, :]
        for pc in range(6):
            wv, sk = stage[pc % 2], skeys[pc % 2]
            dma("sp", wv, w_ada[:, pc * 512:(pc + 1) * 512].rearrange("(k p) n -> p k n", p=128), "wa%d" % (pc % 2), writes=sk)
            if pc < 4:
                for cbk in range(4):
                    blk = pc * 4 + cbk

                    def mm(e, wv=wv, cbk=cbk):
                        r = None
                        for kc in range(8):
                            r = e.matmul(psMi[:, 8 * cbk:8 * cbk + 5], lhsT=wv[:, kc, cbk * 128:(cbk + 1) * 128], rhs=scT[:, kc, :],
                                         start=(kc == 0), stop=(kc == 7))
                        return r
                    P.op("pe", mm, reads=sk + ["scT"], writes=[KMI], cost=3.5)
                    P.op("dve", (lambda e, blk=blk, cbk=cbk: e.tensor_scalar(out=modfm[:, blk, :], in0=psMi[:, 8 * cbk:8 * cbk + 5],
                                                                          scalar1=badaf[:, blk:blk + 1], scalar2=None, op0=ALU.add)),
                         reads=[KMI, *SCK], writes=["modfm"])
            else:
                def mm(e, wv=wv):
                    r = None
                    for kc in range(8):
                        r = e.matmul(psMi[0:5, :], lhsT=scT[:, kc, :], rhs=wv[:, kc, :], start=(kc == 0), stop=(kc == 7))
                    return r
                P.op("pe", mm, reads=sk + ["scT"], writes=[KMI], cost=8.0)
                P.op("dve", (lambda e, pc=pc: e.tensor_copy(out=gate5[:, (pc - 4) * 512:(pc - 3) * 512], in_=psMi[0:5, :])),
                     reads=[KMI], writes=["H"])
        P.op("dve", lambda e: e.tensor_scalar(out=gm[:], in0=modfm[:, 8:16, :], scalar1=1.0, scalar2=None, op0=ALU.add),
             reads=["modfm"], writes=["gm"])
        P.op("dve", lambda e: e.tensor_tensor(out=gm[:], in0=gm[:], in1=gpre[:, :, None].to_broadcast([128, 8, 5]), op=ALU.mult),
             reads=["gm", *SCK], writes=["gm"])
        dma("sp", GG[0:5, :], bada_g, "g5a", writes=["GG"])
        dma("sp", res[0:5, :], gpost5, "g5b", writes=["res"])
        GR5 = tmpo[0:5, :]
        P.op("dve", lambda e: e.tensor_tensor(out=gate5, in0=gate5, in1=GG[0:5, :], op=ALU.add),
             reads=["H", "GG"], writes=["H"])
        P.op("dve", lambda e: e.tensor_tensor(out=GR5, in0=gate5, in1=res[0:5, :], op=ALU.mult),
             reads=["H", "res"], writes=["tmpo"])
        dma("sp", gr5_d, GR5, "g5s", reads=["tmpo"], writes=["gr5d"])

        def build_GG(variant):
            def mm(e):
                r = None
                for cbk in range(2):
                    r = e.matmul(psP[:, cbk * 512:(cbk + 1) * 512], lhsT=sel5[:, variant, :], rhs=GR5[:, cbk * 512:(cbk + 1) * 512],
                                 start=True, stop=True)
                return r
            P.op("pe", mm, reads=["sel5", "tmpo"], writes=KP)
            P.op("act", lambda e: e.copy(out=GG[:], in_=psP[:]), reads=KP, writes=["GG"])

        build_GG(0)

        for pi, (c0, c1) in enumerate(pieces):
            if pi >= 3:
                dma("pool", win[:, :, c0:c1], w_in_v[:, :, c0:c1], "win%d" % pi, writes=["win%d" % pi])
        w_out_v = w_out.rearrange("(k p) n -> p k n", p=128)
        for pi in range(4):
            dma("pool", wout[:, pi * 4:(pi + 1) * 4, :], w_out_v[:, pi * 4:(pi + 1) * 4, :], "wout%d" % pi, writes=["wout%d" % pi])

        for v in (1, 0):
            wv = tmpo[:].rearrange("p (g n) -> p g n", n=128)
            dma("sp", wv, wsT_d[v], "wsT", writes=["tmpo"])
            mk = cmF[:, 1 + 2 * v, :]
            P.op("dve", (lambda e, wv=wv, mk=mk: e.tensor_tensor(out=WsT[:], in0=wv, in1=mk[:, None, :].to_broadcast([128, 8, 128]),
                                                                 op=ALU.mult)), reads=["tmpo", "cmF"], writes=["WsT0"])
            dma("sp", rowsB[33:34, :, :], bs_d[v], "bsr", writes=["res"])
            dma("sp", rowsA[32:34, :], brow_d, "brr", writes=["H"])
            for cbk in range(2):
                P.op("pe", (lambda e, cbk=cbk: e.matmul(psMi[32:33, :], lhsT=cmB[:, 5, 0:1], rhs=WsT[:, cbk * 4:(cbk + 1) * 4, :],
                                                        start=True, stop=True, tile_position=(0, 32))),
                     reads=["cmB", "WsT0"], writes=[KMI])
                P.op("act", (lambda e, cbk=cbk: e.copy(out=rowsB[32:33, cbk * 4:(cbk + 1) * 4, :],
                                                       in_=psMi[32:33, :].rearrange("p (g t) -> p g t", g=4))),
                     reads=[KMI], writes=["res"])

            def mmr(e):
                r = None
                for g in range(8):
                    r = e.matmul(psQ[:, g * 128:(g + 1) * 128], lhsT=rowsA[32:34, g * 128:(g + 1) * 128], rhs=rowsB[32:34, g, :],
                                 start=True, stop=True)
                return r
            P.op("pe", mmr, reads=["H", "res"], writes=KQ)
            P.op("act", lambda e: e.copy(out=Rhl[:, 0, :, :], in_=psQ[:].rearrange("p (g t) -> p g t", g=8)), reads=KQ, writes=["Rhl"])
            if v == 1:
                dma("sp", wss_d, WsT[:], "wsss", reads=["WsT0"], writes=["wssd"])
                dma("sp", rhs_d, Rhl[:], "rhss", reads=["Rhl"], writes=["rhsd"])
        dma("sp", tmpo[:], gvB_d, "gvl", writes=["tmpo"])
        P.op("dve", lambda e: e.tensor_copy(out=gvB[:], in_=tmpo[:]), reads=["tmpo"], writes=["gvB"])

        P.op("pool", lambda e: e.memset(H[:], 0.0), writes=["H"])
        P.op("pool", lambda e: e.memset(Hbf[:], 0.0), writes=["Hbf"])
        P.op("pool", lambda e: e.memset(halo[:], 0.0), writes=HK)

        flags = set()
        pair_i = [0]
        conv_i = [0]

        def next_slot():
            s_ = pair_i[0] % 2
            pair_i[0] += 1
            return s_

        def wait(*fl):
            while not all(f in flags for f in fl):
                yield

        BS0 = dict(hT=hT, hTk="hT", bufA=bufA, bufAk="bufA", BCf=BCf, BCfk="BCf")
        BS1 = dict(hT=hT2, hTk="hT2", bufA=bufA2, bufAk="bufA2", BCf=BCf2, BCfk="BCf2")

        def modeinfo(mode, S):
            smp = mode == "sample"
            v = 1 if smp else 0
            TS = 128 * S
            nseg, Ls = (4, 32) if smp else (1, TS)
            return smp, v, TS, nseg, Ls

        def gen_A(mode, xsrc, S, bs, pre=(), tag=None):
            smp, v, TS, nseg, Ls = modeinfo(mode, S)
            hT_, hTk = bs["hT"], bs["hTk"]
            for j in range(S):
                dma("sp", xin0[:], xsrc[j * 128:(j + 1) * 128, :], "xl0", writes=["xin0"])
                P.op("act", lambda e: e.activation(out=xs[:], in_=xin0[:], func=AF.Square, accum_out=small[:, 0:1]),
                     reads=["xin0"], writes=["xs", "sm_ss"], cost=1.0)
                P.op("dve", lambda e: e.tensor_scalar(out=small[:, 1:2], in0=small[:, 0:1], scalar1=1.0 / D, scalar2=EPS,
                                                      op0=ALU.mult, op1=ALU.add), reads=["sm_ss"], writes=["sm_var"])
                P.op("pool", lambda e: e.tensor_tensor(out=small[:, 2:3], in0=small[:, 1:2], in1=mhalf, op=ALU.pow),
                     reads=["sm_var", *SCK], writes=["sm_rstd"])
                P.op("dve", lambda e: e.tensor_scalar(out=xs[:], in0=xin0[:], scalar1=small[:, 2:3], scalar2=None, op0=ALU.mult),
                     reads=["xin0", "sm_rstd"], writes=["xs"], cost=0.8)
                yield
                yield from wait(*pre)

                def tr(e):
                    r = None
                    for kc in range(8):
                        r = e.transpose(out=psTX[:, kc * 128:(kc + 1) * 128], in_=xs[:, kc * 128:(kc + 1) * 128], identity=identB)
                    return r
                P.op("pe", tr, reads=["xs", "cmB"], writes=[KTX], cost=0.8)
                for kc in range(8):
                    if smp:
                        for sq in range(4):
                            P.op("act", (lambda e, kc=kc, sq=sq: e.activation(
                                out=hT_[:, kc, sq * 32:(sq + 1) * 32], in_=psTX[:, kc * 128 + sq * 32: kc * 128 + (sq + 1) * 32],
                                func=AF.Identity, scale=gm[:, kc, 1 + sq:2 + sq], bias=modfm[:, kc, 1 + sq:2 + sq])),
                                reads=[KTX, "gm", "modfm"], writes=[hTk])
                    elif mode == "fast" and kc % 2 == 1:
                        P.op("dve", (lambda e, kc=kc, j=j: e.tensor_scalar(
                            out=hT_[:, kc, j * 128:(j + 1) * 128], in0=psTX[:, kc * 128:(kc + 1) * 128],
                            scalar1=gm[:, kc, 0:1], scalar2=modfm[:, kc, 0:1], op0=ALU.mult, op1=ALU.add)),
                            reads=[KTX, "gm", "modfm"], writes=[hTk], cost=0.3)
                    else:
                        P.op("act", (lambda e, kc=kc, j=j: e.activation(
                            out=hT_[:, kc, j * 128:(j + 1) * 128], in_=psTX[:, kc * 128:(kc + 1) * 128],
                            func=AF.Identity, scale=gm[:, kc, 0:1], bias=modfm[:, kc, 0:1])),
                            reads=[KTX, "gm", "modfm"], writes=[hTk])
                yield
            if tag is not None:
                flags.add(tag)

        def fm_pair(col0, slot, TS, bs):
            hT_, hTk = bs["hT"], bs["hTk"]

            def mm(e):
                r = None
                for c in range(2):
                    for kc in range(8):
                        r = e.matmul(psF[slot][:, c * TS:(c + 1) * TS], lhsT=win[:, kc, col0 + c * 128: col0 + (c + 1) * 128],
                                     rhs=hT_[:, kc, 0:TS], start=(kc == 0), stop=(kc == 7))
                return r
            P.op("pe", mm, reads=[wkey(col0), wkey(col0 + 255), hTk], writes=[KF[slot]], cost=0.15 * TS / 16)

        def gen_Bx(mode, S, bs, need_c=True, pre=(), tag=None):
            smp, v, TS, nseg, Ls = modeinfo(mode, S)
            full = mode != "fast"
            yield from wait(*pre)
            for t0 in [0, 2, 4, 6, 8] + ([10] if (full or need_c) else []):
                slot = next_slot()
                fm_pair(C_XBC + t0 * 128, slot, TS, bs)
                xr, xrk = XR[slot], "XR%d" % slot
                xrv = xr[:, :, 0:nseg * (3 + Ls)].rearrange("p c (s l) -> p c s l", l=3 + Ls)
                P.op("act", (lambda e, xrv=xrv, t0=t0: e.copy(out=xrv[:, :, :, 0:3], in_=halo[:, t0:t0 + 2, 0:nseg, :])),
                     reads=["halo%d" % t0], writes=[xrk + "h"], cost=0.25)
                P.op("act", (lambda e, xrv=xrv, slot=slot: e.copy(
                    out=xrv[:, :, :, 3:3 + Ls], in_=psF[slot][:, 0:2 * TS].rearrange("p (c s l) -> p c s l", c=2, l=Ls))),
                    reads=[KF[slot]], writes=[xrk])
                P.op("act", (lambda e, xrv=xrv, t0=t0: e.copy(out=halo[:, t0:t0 + 2, 0:nseg, :], in_=xrv[:, :, :, Ls:Ls + 3])),
                     reads=[xrk], writes=["halo%d" % t0], cost=0.25)
                for c in range(2):
                    ti = t0 + c
                    ceng = "dve"
                    conv_i[0] += 1
                    ac, ack = acc[c], "acc%d" % c
                    acv = ac[:, 0:TS].rearrange("p (s l) -> p s l", l=Ls)
                    if full:
                        P.op("act", (lambda e, acv=acv, slot=slot, c=c, ti=ti: e.activation(
                            out=acv, in_=psF[slot][:, c * TS:(c + 1) * TS].rearrange("p (s l) -> p s l", l=Ls), func=AF.Identity,
                            scale=cw[:, ti, 3:4], bias=cb[:, ti:ti + 1])),
                            reads=[KF[slot], *SCK], writes=[ack], cost=0.45)
                    else:
                        P.op("dve", (lambda e, acv=acv, xrv=xrv, c=c, ti=ti: e.tensor_scalar(
                            out=acv, in0=xrv[:, c, :, 3:3 + Ls], scalar1=cw[:, ti, 3:4], scalar2=cb[:, ti:ti + 1], op0=ALU.mult, op1=ALU.add)),
                            reads=[xrk, xrk + "h", *SCK], writes=[ack])
                    for k in range(3):
                        P.op(ceng, (lambda e, acv=acv, xrv=xrv, c=c, ti=ti, k=k: e.scalar_tensor_tensor(
                            out=acv, in0=xrv[:, c, :, k:k + Ls], scalar=cw[:, ti, k:k + 1], in1=acv, op0=ALU.mult, op1=ALU.add)),
                            reads=[xrk, xrk + "h", ack, *SCK], writes=[ack])
                    if ti < 8:
                        dst, dk = bs["bufA"][:, ti, 0:TS], bs["bufAk"]
                    else:
                        dst, dk = bs["BCf"][:, ti - 8, 0:TS], bs["BCfk"]
                    P.op("act", (lambda e, ac=ac, dst=dst: e.activation(out=dst, in_=ac[:, 0:TS], func=AF.Silu)),
                         reads=[ack], writes=[dk], cls="S")
                yield
            if tag is not None:
                flags.add(tag)

        def gen_Bzug(mode, S, bs, pre=(), tagz=None, tagug=None):
            smp, v, TS, nseg, Ls = modeinfo(mode, S)
            yield from wait(*pre)
            for t0 in range(0, 8, 2):
                slot = next_slot()
                fm_pair(C_Z + t0 * 128, slot, TS, bs)
                P.op("act", (lambda e, slot=slot, t0=t0: e.activation(
                    out=bufB[:, t0:t0 + 2, 0:TS], in_=psF[slot][:, 0:2 * TS].rearrange("p (c t) -> p c t", c=2), func=AF.Silu)),
                    reads=[KF[slot]], writes=["bufB"], cls="S")
                yield
            flags.add(tagz)
            for t0 in range(0, 8, 2):
                slot = next_slot()
                fm_pair(C_G + t0 * 128, slot, TS, bs)
                P.op("act", (lambda e, slot=slot: e.activation(
                    out=sgt[:, :, 0:TS], in_=psF[slot][:, 0:2 * TS].rearrange("p (c t) -> p c t", c=2), func=AF.Silu)),
                    reads=[KF[slot]], writes=["sgt"], cls="S")
                slot = next_slot()
                fm_pair(C_U + t0 * 128, slot, TS, bs)
                P.op("dve", (lambda e, slot=slot, t0=t0: e.tensor_tensor(
                    out=bufC[:, t0:t0 + 2, 0:TS], in0=psF[slot][:, 0:2 * TS].rearrange("p (c t) -> p c t", c=2), in1=sgt[:, :, 0:TS],
                    op=ALU.mult)), reads=[KF[slot], "sgt"], writes=["bufC"])
                yield
            flags.add(tagug)

        def gen_C(mode, j, bs, ts, pre=(), pre_z=(), pre_state=(), pre_y=(), tag_state=None, tag=None):
            smp = mode == "sample"
            full = mode != "fast"
            v = 1 if smp else 0
            maskF, m1F = cmF[:, 1 + 2 * v, :], cmF[:, 2 + 2 * v, :]
            maskB, m1B = cmB[:, 1 + 2 * v, :], cmB[:, 2 + 2 * v, :]
            onesF = cmF[:, 5, :]
            hT_, hTk = bs["hT"], bs["hTk"]
            bufA_, bufAk, BCf_, BCfk = bs["bufA"], bs["bufAk"], bs["BCf"], bs["BCfk"]
            js = slice(j * 128, (j + 1) * 128)
            sm = small[:, 0:192]
            K = lambda n: "sm%d_%s" % (ts, n)
            LX_, XDD_, ZS_, Btm_, mCBT_ = LX[ts], XDD[ts], ZS[ts], Btm[ts], mCBT[ts]
            LXk, XDDk, ZSk, Btmk, mCBTk = "LX0", "XDD0", "ZS0", "Btm0", "mCBT0"
            dtr, dte, dt_, dA = sm[:, 16:32], sm[:, 32:48], sm[:, 48:64], sm[:, 64:80]
            ntot = 4 if smp else 1
            eae = sm[:, 80:80 + 32 + 16 * ntot]
            ea, dend = sm[:, 80:96], sm[:, 96:112]
            dtd = sm[:, 176:192]
            yield from wait(*pre)

            def mmdt(e):
                r = None
                for kc in range(8):
                    r = e.matmul(psMi[:, 0:16], lhsT=hT_[:, kc, js], rhs=win[:, kc, C_DT:C_DT + 16], start=(kc == 0), stop=(kc == 7))
                return r
            P.op("pe", mmdt, reads=[hTk, wkey(C_DT)], writes=[KMI])
            P.op("dve", lambda e: e.tensor_tensor(out=dtr, in0=psMi[:, 0:16], in1=dtb, op=ALU.add),
                 reads=[KMI, *SCK], writes=[K("dtr")])
            yield
            P.op("act", lambda e: e.activation(out=dte, in_=dtr, func=AF.Exp), reads=[K("dtr")], writes=[K("dte")], cost=1.0, cls="E")
            P.op("act", lambda e: e.activation(out=dt_, in_=dte, func=AF.Ln, bias=1.0), reads=[K("dte")], writes=[K("dt")], cost=1.0, cls="E")
            P.op("dve", lambda e: e.tensor_tensor(out=dA, in0=dt_, in1=Abc, op=ALU.mult),
                 reads=[K("dt"), *SCK], writes=[K("dA")])
            yield
            if smp:
                dma("sp", accT[:].rearrange("p a (q n) -> p (a q) n", n=128), segsel_d, "segsel", writes=["acc0", "acc1"])

            def mme(e):
                e.matmul(psMi[:, 16:32], lhsT=maskF, rhs=dA, start=True, stop=True)
                r = e.matmul(psMi[:, 32:48], lhsT=m1F, rhs=dA, start=True, stop=True)
                for q in range(ntot):
                    lt = segsel[:, q, :] if smp else onesF
                    r = e.matmul(psMi[:, 48 + 16 * q:64 + 16 * q], lhsT=lt, rhs=dA, start=True, stop=True)
                return r
            P.op("pe", mme, reads=[K("dA"), "cmF", "acc0", "acc1"] if smp else [K("dA"), "cmF"], writes=[KMI])
            P.op("act", lambda e: e.activation(out=eae, in_=psMi[:, 16:48 + 16 * ntot], func=AF.Exp),
                 reads=[KMI], writes=[K("eae")], cost=0.8, cls="E")
            P.op("dve", lambda e: e.tensor_tensor(out=dtd, in0=dt_, in1=dend, op=ALU.mult),
                 reads=[K("dt"), K("eae")], writes=[K("dtd")])
            yield

            def trx(e):
                r = None
                for i in range(8):
                    r = e.transpose(out=psTX[:, i * 128:(i + 1) * 128], in_=bufA_[:, i, js], identity=identB)
                return r
            P.op("pe", trx, reads=[bufAk, "cmB"], writes=[KTX], cost=0.8)
            psXv = psTX[:].rearrange("p (h q) -> p h q", q=64)
            if full:
                P.op("dve", lambda e: e.tensor_tensor(out=LX_[:].rearrange("p (h q) -> p h q", q=64), in0=psXv,
                                                      in1=dt_[:, :, None].to_broadcast([128, 16, 64]), op=ALU.mult),
                     reads=[KTX, K("dt")], writes=[LXk], cost=1.2)
            P.op("dve", lambda e: e.tensor_tensor(out=XDD_[:].rearrange("p (h q) -> p h q", q=64), in0=psXv,
                                                  in1=dtd[:, :, None].to_broadcast([128, 16, 64]), op=ALU.mult),
                 reads=[KTX, K("dtd")], writes=[XDDk], cost=1.2)
            yield
            psBv = psMi[:, 384:512].bitcast(BF16)

            def trb(e):
                r = None
                for g in range(2):
                    r = e.transpose(out=psBv[:, g * 128:(g + 1) * 128], in_=BCf_[:, g, js], identity=identB)
                return r
            P.op("pe", trb, reads=[BCfk, "cmB"], writes=[KMI])
            P.op("act", lambda e: e.copy(out=Btm_[:], in_=psBv), reads=[KMI], writes=[Btmk])
            yield

            if full:
                def mmcb(e):
                    r = None
                    for g in range(2):
                        r = e.matmul(psMi[:, 128 + g * 128:256 + g * 128], lhsT=BCf_[:, g, js], rhs=BCf_[:, 2 + g, js], start=True, stop=True)
                    return r
                P.op("pe", mmcb, reads=[BCfk], writes=[KMI])
                P.op("dve", lambda e: e.tensor_tensor(out=mCBT_[:], in0=psMi[:, 128:384].rearrange("p (g l) -> p g l", g=2),
                                                      in1=maskF[:, None, :].to_broadcast([128, 2, 128]), op=ALU.mult),
                     reads=[KMI, "cmF"], writes=[mCBTk])
                yield
                yield from wait(*pre_z)
                if smp:
                    for sq in range(4):
                        dma("sp", H[:], sssm_d[sq], "hl", writes=["H"])
                        P.op("act", lambda e: e.copy(out=Hbf[:], in_=H[:]), reads=["H"], writes=["Hbf"], cost=1.05)

                        def mmz(e, sq=sq):
                            r = None
                            for g in range(2):
                                r = e.matmul(psP[32 * sq:32 * sq + 32, g * 512:(g + 1) * 512], lhsT=BCf_[:, 2 + g, 32 * sq:32 * sq + 32],
                                             rhs=Hbf[:, g * 512:(g + 1) * 512], start=True, stop=True, tile_position=(0, 32 * sq))
                            return r
                        P.op("pe", mmz, reads=[BCfk, "Hbf"], writes=KP, cost=0.6)
                else:
                    def mmz(e):
                        r = None
                        for g in range(2):
                            r = e.matmul(psP[:, g * 512:(g + 1) * 512], lhsT=BCf_[:, 2 + g, js], rhs=Hbf[:, g * 512:(g + 1) * 512],
                                         start=True, stop=True)
                        return r
                    P.op("pe", mmz, reads=[BCfk, "Hbf"], writes=KP, cost=0.6)
                P.op("dve", lambda e: e.tensor_tensor(out=ZS_[:].rearrange("p (h q) -> p h q", q=64),
                                                      in0=psP[:].rearrange("p (h q) -> p h q", q=64),
                                                      in1=ea[:, :, None].to_broadcast([128, 16, 64]), op=ALU.mult),
                     reads=KP + [K("eae")], writes=[ZSk], cost=1.15)
                yield

            yield from wait(*pre_state)
            if smp:
                for sq in range(4):
                    rows = slice(32 * sq, 32 * sq + 32)

                    def mms(e, rows=rows, sq=sq):
                        r = None
                        for g in range(2):
                            r = e.matmul(psQ[:, g * 512:(g + 1) * 512], lhsT=Btm_[rows, g * 128:(g + 1) * 128],
                                         rhs=XDD_[rows, g * 512:(g + 1) * 512], start=True, stop=True, tile_position=(32 * sq, 0))
                        return r
                    P.op("pe", mms, reads=[Btmk, XDDk], writes=KQ, cost=0.6)
                    dma("sp", H[:], sssm_d[sq], "hl", writes=["H"])
                    dec = sm[:, 112 + 16 * sq:128 + 16 * sq]
                    P.op("dve", (lambda e, dec=dec: e.tensor_tensor(out=H[:].rearrange("p (h q) -> p h q", q=64),
                                                                    in0=H[:].rearrange("p (h q) -> p h q", q=64),
                                                                    in1=dec[:, :, None].to_broadcast([128, 16, 64]), op=ALU.mult)),
                         reads=["H", K("eae")], writes=["H"], cost=1.9)
                    P.op("dve", lambda e: e.tensor_tensor(out=H[:], in0=H[:], in1=psQ[:], op=ALU.add),
                         reads=["H"] + KQ, writes=["H"], cost=1.15)
                    dma("sp", ssms_o[sq], H[:], "hs", reads=["H"])
            else:
                def mms(e):
                    r = None
                    for g in range(2):
                        r = e.matmul(psQ[:, g * 512:(g + 1) * 512], lhsT=Btm_[:, g * 128:(g + 1) * 128],
                                     rhs=XDD_[:, g * 512:(g + 1) * 512], start=True, stop=True)
                    return r
                P.op("pe", mms, reads=[Btmk, XDDk], writes=KQ, cost=0.6)
                dec = sm[:, 112:128]
                P.op("dve", lambda e: e.tensor_tensor(out=H[:].rearrange("p (h q) -> p h q", q=64),
                                                      in0=H[:].rearrange("p (h q) -> p h q", q=64),
                                                      in1=dec[:, :, None].to_broadcast([128, 16, 64]), op=ALU.mult),
                     reads=["H", K("eae")], writes=["H"], cost=1.15)
                P.op("dve", lambda e: e.tensor_tensor(out=H[:], in0=H[:], in1=psQ[:], op=ALU.add),
                     reads=["H"] + KQ, writes=["H"], cost=1.15)
                if full:
                    P.op("act", lambda e: e.copy(out=Hbf[:], in_=H[:]), reads=["H"], writes=["Hbf"], cost=1.05)
            if tag_state is not None:
                flags.add(tag_state)
            yield

            if full:
                yield from wait(*pre_y)
                for hh in range(2):
                    P.op("dve", (lambda e, hh=hh: e.tensor_tensor(
                        out=M2[:], in0=dA[:, hh * 8:(hh + 1) * 8, None].to_broadcast([128, 8, 128]),
                        in1=maskB[:, None, :].to_broadcast([128, 8, 128]), op=ALU.mult)),
                        reads=[K("dA"), "cmB"], writes=["M2"], cost=1.15)
                    for q in range(2):
                        slot = next_slot()
                        P.op("pe", (lambda e, slot=slot, q=q: e.matmul(psF[slot][:, :], lhsT=m1B, rhs=M2[:, q * 4:(q + 1) * 4, :],
                                                                       start=True, stop=True)),
                             reads=["M2", "cmB"], writes=[KF[slot]])
                        P.op("act", (lambda e, slot=slot, q=q: e.activation(
                            out=EW[:, q * 4:(q + 1) * 4, :], in_=psF[slot][:, :].rearrange("p (h l) -> p h l", h=4), func=AF.Exp)),
                            reads=[KF[slot]], writes=["EW0"], cost=0.7, cls="E")
                    P.op("dve", (lambda e, hh=hh: e.tensor_tensor(
                        out=EW[:], in0=EW[:], in1=mCBT_[:, hh:hh + 1, :].to_broadcast([128, 8, 128]), op=ALU.mult)),
                        reads=["EW0", mCBTk], writes=["EW0"], cost=1.15)

                    def mmy(e, hh=hh):
                        r = None
                        for hl in range(8):
                            h = hh * 8 + hl
                            i, half = h // 2, h % 2
                            o_ = psQ[64 * half:64 * half + 64, i * 128:(i + 1) * 128]
                            e.matmul(o_, lhsT=LX_[:, h * 64:(h + 1) * 64], rhs=EW[:, hl, :], start=True, stop=False,
                                     tile_position=(0, 64 * half))
                            r = e.matmul(o_, lhsT=ZS_[:, h * 64:(h + 1) * 64], rhs=identB, start=False, stop=True,
                                         tile_position=(0, 64 * half))
                        return r
                    P.op("pe", mmy, reads=[LXk, ZSk, "EW0", "cmB"], writes=[KQ[hh]], cost=1.5)
                ydv = tmpo[:].rearrange("p (i l) -> p i l", l=128)
                for i in range(8):
                    P.op("dve", (lambda e, i=i: e.scalar_tensor_tensor(
                        out=ydv[:, i, :], in0=bufA_[:, i, js], scalar=Dp[:, i:i + 1], in1=psQ[:, i * 128:(i + 1) * 128],
                        op0=ALU.mult, op1=ALU.add)), reads=[bufAk, *SCK, KQ[i // 4]], writes=["tmpo"])
                P.op("dve", lambda e: e.tensor_tensor(out=bufB[:, :, js], in0=ydv, in1=bufB[:, :, js], op=ALU.mult),
                     reads=["tmpo", "bufB"], writes=["bufB"], cost=1.2)
                yield
            if tag is not None:
                flags.add(tag)

        def gen_D(mode, j, bs, ts, pre=(), tag_v=None, tag=None):
            smp = mode == "sample"
            hT_, hTk = bs["hT"], bs["hTk"]
            js = slice(j * 128, (j + 1) * 128)
            sm = small[:, 192:256]
            K = lambda n: "sm%d_%s" % (ts, n)
            yield from wait(*pre)

            def mmv(e):
                r = None
                for kc in range(8):
                    for cbk in range(2):
                        r = e.matmul(psP[:, cbk * 512:(cbk + 1) * 512], lhsT=hT_[:, kc, js],
                                     rhs=win[:, kc, C_V + cbk * 512:C_V + (cbk + 1) * 512], start=(kc == 0), stop=(kc == 7))
                return r
            P.op("pe", mmv, reads=[hTk, wkey(C_V), wkey(C_V + 512)], writes=KP, cost=4.8)
            if tag_v is not None:
                flags.add(tag_v)
            for cbk in range(2):
                P.op("dve", (lambda e, cbk=cbk: e.bn_stats(out=stats[:, cbk, :], in_=psP[:, cbk * 512:(cbk + 1) * 512])),
                     reads=[KP[cbk]], writes=["stats"])
            P.op("dve", lambda e: e.bn_aggr(out=sm[:, 4:6], in_=stats[:]), reads=["stats"], writes=[K("mv")])
            P.op("dve", lambda e: e.tensor_scalar(out=sm[:, 6:7], in0=sm[:, 5:6], scalar1=EPS, scalar2=None, op0=ALU.add),
                 reads=[K("mv")], writes=[K("v2")])
            P.op("pool", lambda e: e.tensor_tensor(out=sm[:, 7:8], in0=sm[:, 6:7], in1=mhalf, op=ALU.pow),
                 reads=[K("v2"), *SCK], writes=[K("vr")])
            P.op("dve", lambda e: e.scalar_tensor_tensor(out=sm[:, 8:9], in0=sm[:, 4:5], scalar=-1.0, in1=sm[:, 7:8],
                                                         op0=ALU.mult, op1=ALU.mult), reads=[K("mv"), K("vr")], writes=[K("nmr")])
            if smp:
                P.op("act", lambda e: e.activation(out=tmpo[:], in_=psP[:], func=AF.Identity, scale=sm[:, 7:8], bias=sm[:, 8:9]),
                     reads=KP + [K("vr"), K("nmr")], writes=["tmpo"])
                dma("sp", xin0[:], gvB_d, "gvl2", writes=["xin0"])
                dma("sp", res[:], bvB_d, "bvl2", writes=["res"])
                P.op("dve", lambda e: e.tensor_tensor(out=tmpo[:], in0=tmpo[:], in1=xin0[:], op=ALU.mult),
                     reads=["tmpo", "xin0"], writes=["tmpo"])
                P.op("dve", lambda e: e.tensor_tensor(out=tmpo[:], in0=tmpo[:], in1=res[:], op=ALU.add),
                     reads=["tmpo", "res"], writes=["tmpo"])
                dma("sp", vns_o, tmpo[:], "vno", reads=["tmpo"])
            P.op("act", lambda e: e.activation(out=vhat[:], in_=psP[:], func=AF.Identity, scale=sm[:, 7:8], bias=sm[:, 8:9]),
                 reads=KP + [K("vr"), K("nmr")], writes=["vhat"], cost=1.05)
            P.op("dve", lambda e: e.tensor_tensor(out=vhat[:], in0=vhat[:], in1=gvB[:], op=ALU.mult),
                 reads=["vhat", "gvB"], writes=["vhat"], cost=1.0)

            def mmm(e):
                r = None
                for g in range(8):
                    o_ = psQ[:, g * 128:(g + 1) * 128]
                    e.matmul(o_, lhsT=vhat[:, g * 128:(g + 1) * 128], rhs=WsT[:, g, :], start=True, stop=False)
                    r = e.matmul(o_, lhsT=identB, rhs=Rhl[:, 0, g, :], start=False, stop=True)
                return r
            P.op("pe", mmm, reads=["vhat", "WsT0", "Rhl", "cmB"], writes=KQ, cost=1.2)
            P.op("dve", lambda e: e.tensor_tensor(out=bufC[:, :, js], in0=psQ[:].rearrange("p (g t) -> p g t", g=8),
                                                  in1=bufC[:, :, js], op=ALU.mult),
                 reads=KQ + ["bufC"], writes=["bufC"], cost=1.2)
            yield
            if tag is not None:
                flags.add(tag)

        def gen_E(mode, j, xsrc, ydst, ts, tag=None):
            js = slice(j * 128, (j + 1) * 128)
            sm = small[:, 192:256]
            K = lambda n: "sm%d_%s" % (ts, n)

            def mmo(e):
                r = None
                for kc in range(16):
                    src = bufB if kc < 8 else bufC
                    for cbk in range(2):
                        r = e.matmul(psP[:, cbk * 512:(cbk + 1) * 512], lhsT=src[:, kc % 8, js],
                                     rhs=wout[:, kc, cbk * 512:(cbk + 1) * 512], start=(kc == 0), stop=(kc == 15))
                return r
            P.op("pe", mmo, reads=["bufB", "bufC", "wout0", "wout1", "wout2", "wout3"], writes=KP, cost=9.3)
            dma("sp", res[:], xsrc[j * 128:(j + 1) * 128, :], "rl", writes=["res"])
            P.op("act", lambda e: e.activation(out=tmpo[:], in_=psP[:], func=AF.Square, accum_out=sm[:, 9:10]),
                 reads=KP, writes=["tmpo", K("ss2")], cost=1.0)
            P.op("dve", lambda e: e.tensor_scalar(out=sm[:, 10:11], in0=sm[:, 9:10], scalar1=1.0 / D, scalar2=EPS,
                                                  op0=ALU.mult, op1=ALU.add), reads=[K("ss2")], writes=[K("var2")])
            P.op("pool", lambda e: e.tensor_tensor(out=sm[:, 11:12], in0=sm[:, 10:11], in1=mhalf, op=ALU.pow),
                 reads=[K("var2"), *SCK], writes=[K("rstd2")])
            P.op("dve", lambda e: e.scalar_tensor_tensor(out=tmpo[:], in0=psP[:], scalar=sm[:, 11:12], in1=GG[:],
                                                         op0=ALU.mult, op1=ALU.mult), reads=KP + [K("rstd2"), "GG"], writes=["tmpo"], cost=1.2)
            P.op("dve", lambda e: e.tensor_tensor(out=res[:], in0=res[:], in1=tmpo[:], op=ALU.add),
                 reads=["res", "tmpo"], writes=["res"], cost=1.15)
            dma("sp", ydst[j * 128:(j + 1) * 128, :], res[:], "yo", reads=["res"])
            yield
            if tag is not None:
                flags.add(tag)

        def chain(*gens):
            for g in gens:
                yield from g

        def rr(*gens):
            gens = list(gens)
            guard = 0
            while gens:
                progressed = False
                n0 = len(P.ops)
                for g in list(gens):
                    try:
                        next(g)
                    except StopIteration:
                        gens.remove(g)
                        progressed = True
                if len(P.ops) != n0:
                    progressed = True
                guard = 0 if progressed else guard + 1
                assert guard < 50, "scheduling deadlock"

        nf = n_fast_st
        bsets = [BS0, BS1]

        def fast_ab():
            for i in range(nf):
                bs = bsets[i % 2]
                pre = [("fc", i - 2)] if i >= 2 else []
                yield from gen_A("fast", xp[i * 256:(i + 1) * 256, :], 2, bs, pre=pre)
                yield from gen_Bx("fast", 2, bs, need_c=(i == nf - 1), pre=pre, tag=("fb", i))

        def fast_c():
            for i in range(nf):
                bs = bsets[i % 2]
                for j in range(2):
                    yield from gen_C("fast", j, bs, 0, pre=[("fb", i)])
                flags.add(("fc", i))

        rr(fast_ab(), fast_c())
        P.op("dve", lambda e: e.tensor_scalar(out=H[:], in0=H[:], scalar1=flg, scalar2=None, op0=ALU.mult),
             reads=["H", *SCK], writes=["H"])
        P.op("act", lambda e: e.copy(out=Hbf[:], in_=H[:]), reads=["H"], writes=["Hbf"], cost=1.05)
        P.op("dve", lambda e: e.tensor_scalar(out=halo[:], in0=halo[:], scalar1=flg, scalar2=None, op0=ALU.mult),
             reads=HK + [*SCK], writes=HK)

        nm = n_full_st

        def full_ab():
            for i in range(nm):
                bs = bsets[i % 2]
                xsrc = xc[i * 256:(i + 1) * 256, :]
                pre2 = [("dv", i - 2, 0), ("dv", i - 2, 1), ("cdone", i - 2, 0), ("cdone", i - 2, 1)] if i >= 2 else []
                yield from gen_A("full", xsrc, 2, bs, pre=pre2)
                yield from gen_Bx("full", 2, bs, pre=pre2, tag=("bx", i))
                preZ = [("edone", i - 1, 0), ("edone", i - 1, 1)] if i > 0 else []
                yield from gen_Bzug("full", 2, bs, pre=preZ, tagz=("bz", i), tagug=("bug", i))

        def full_c():
            for i in range(nm):
                bs = bsets[i % 2]
                for j in range(2):
                    yield from gen_C("full", j, bs, 0, pre=[("bx", i)], pre_y=[("bz", i)], tag=("cdone", i, j))

        def full_de():
            for i in range(nm):
                bs = bsets[i % 2]
                xsrc = xc[i * 256:(i + 1) * 256, :]
                ydst = yc[i * 256:(i + 1) * 256, :]
                for j in range(2):
                    yield from gen_D("full", j, bs, 1, pre=[("bug", i)], tag_v=("dv", i, j))
                    yield from wait(("cdone", i, j))
                    yield from gen_E("full", j, xsrc, ydst, 1, tag=("edone", i, j))

        rr(full_ab(), full_c(), full_de())
        dma("sp", ssmp_o, H[:], "hpo", reads=["H"])
        dma("sp", convp_o, halo[:, :, 0, :], "cpo", reads=HK)

        dma("sp", halo[:], sconv_d, "hls", reads=HK, writes=HK)
        dma("sp", tmpo[0:5, :], gr5_d, "g5l", reads=["gr5d"], writes=["tmpo"])
        build_GG(1)
        dma("sp", WsT[:], wss_d, "wssl", reads=["wssd"], writes=["WsT0"])
        dma("sp", Rhl[:], rhs_d, "rhsl", reads=["rhsd"], writes=["Rhl"])
        flags.add("s_go")
        rr(chain(gen_A("sample", xsm, 1, BS0), gen_Bx("sample", 1, BS0), gen_Bzug("sample", 1, BS0, tagz="s_bz", tagug="s_bug"),
                 gen_C("sample", 0, BS0, 0), gen_D("sample", 0, BS0, 0), gen_E("sample", 0, xsm, ysm, 0)))
        dma("sp", convs_o, halo[:], "cso", reads=HK)
        P.finalize()
    return nc


def _consts():
    t = np.arange(128)
    ident = np.eye(128, dtype=np.float32)
    mask_p = (t[:, None] <= t[None, :]).astype(np.float32)
    m1_p = (t[:, None] > t[None, :]).astype(np.float32)
    same = (t[:, None] // 32 == t[None, :] // 32)
    mask_s = (same & (t[:, None] <= t[None, :])).astype(np.float32)
    m1_s = (same & (t[:, None] > t[None, :])).astype(np.float32)
    ones = np.ones((128, 128), np.float32)
    cmask = np.stack([ident, mask_p, m1_p, mask_s, m1_s, ones])
    segsel = np.zeros((128, 4, 128), np.float32)
    for q in range(4):
        segsel[32 * q:32 * q + 32, q, :] = 1.0
    sel5 = np.zeros((2, 5, 128), np.float32)
    sel5[0, 0, :] = 1.0
    for q in range(4):
        sel5[1, 1 + q, 32 * q:32 * q + 32] = 1.0
    return cmask, segsel, sel5


def kernel(x_prompt, x_sample, state_conv, state_ssm, c_prompt, c_sample, w_ada, b_ada, g_pre,
           g_post, w_in, conv_w, conv_b, dt_bias, a_log, d_skip, g_v, beta_v, w_s, b_s, w_out):
    f = lambda a: np.ascontiguousarray(np.asarray(a, dtype=np.float32))
    x_prompt, x_sample = f(x_prompt), f(x_sample)
    B, SEQ, _ = x_prompt.shape
    half = SEQ // 2
    n_st = half // 256
    NB = x_sample.shape[0] // 8
    assert NB == 4 and B == 4 and half % 256 == 0
    nc = build(n_st, n_st)
    cmask, segsel, sel5 = _consts()
    w_ada0, b_ada0, w_in0, w_out0 = f(w_ada)[0], f(b_ada)[0], f(w_in)[0], f(w_out)[0]
    rep = lambda vec, n=128: np.ascontiguousarray(np.broadcast_to(np.asarray(vec, np.float32)[None, :], (n, len(vec))))
    ws0 = f(w_s)[0]
    wsT_p = np.ascontiguousarray(ws0.transpose(2, 0, 1))
    wsT_s = np.zeros((128, 8, 128), np.float32)
    blk = ws0[:, :32, :32].transpose(2, 0, 1)
    for q in range(4):
        wsT_s[32 * q:32 * q + 32, :, 32 * q:32 * q + 32] = blk
    bs0 = f(b_s)[0]
    bs_p = bs0[None]
    bs_s = np.tile(bs0[:, :32], (1, 4))[None]
    shared = dict(
        w_ada=w_ada0, bada_fm=np.ascontiguousarray(b_ada0[:2048].reshape(16, 128).T), bada_g=rep(b_ada0[2048:], 5),
        gpre_fm=np.ascontiguousarray(f(g_pre)[0].reshape(8, 128).T), gpost5=rep(f(g_post)[0], 5),
        w_in=w_in0, w_out=w_out0,
        cw=np.ascontiguousarray(f(conv_w)[0].reshape(4, 12, 128).transpose(2, 1, 0)),
        cb=np.ascontiguousarray(f(conv_b)[0].reshape(12, 128).T),
        dtb=rep(f(dt_bias)[0]), alog=rep(f(a_log)[0]),
        Dp=np.ascontiguousarray(np.repeat(f(d_skip)[0].reshape(8, 2), 64, axis=1).T),
        gvB=rep(f(g_v)[0]), bvB=rep(f(beta_v)[0]),
        brow=np.ascontiguousarray(np.stack([f(beta_v)[0], np.ones(D, np.float32)])),
        wsT=np.ascontiguousarray(np.stack([wsT_p, wsT_s])), bsrow=np.ascontiguousarray(np.stack([bs_p, bs_s])),
        cmask=cmask, segsel=segsel, sel5=sel5,
    )
    sc0, ss0 = f(state_conv)[0], f(state_ssm)[0]
    in_maps = []
    for core in range(8):
        s, hb = core // 2, core % 2
        sq = slice(4 * core, 4 * core + 4)
        m = dict(shared)
        m["xp"] = x_prompt[s, 0:half]
        m["xc"] = x_prompt[s, hb * half:(hb + 1) * half]
        m["xsm"] = np.ascontiguousarray(x_sample[sq].reshape(128, D))
        m["flag"] = np.full((128, 1), float(hb), np.float32)
        crow = np.concatenate([f(c_prompt)[s:s + 1], f(c_sample)[sq]], 0)
        m["cT"] = np.ascontiguousarray(crow.reshape(5, 8, 128).transpose(2, 1, 0))
        m["sconv"] = np.ascontiguousarray(sc0[sq].reshape(4, 3, 12, 128).transpose(3, 2, 0, 1))
        m["sssm"] = np.ascontiguousarray(ss0[sq].reshape(4, D, 128).transpose(0, 2, 1))
        in_maps.append(m)
    out = run_bass_kernel_spmd(nc, in_maps, core_ids=list(range(8)))
    R = out.results
    y_prompt = np.empty((B, SEQ, D), np.float32)
    y_sample = np.empty_like(x_sample)
    conv_prompt = np.empty((1, B, 3, 1536), np.float32)
    ssm_prompt = np.empty((1, B, 16, 64, 128), np.float32)
    conv_sample = np.empty((1, 32, 3, 1536), np.float32)
    ssm_sample = np.empty((1, 32, 16, 64, 128), np.float32)
    vn = np.empty((1, 32, 32, D), np.float32)
    for core in range(8):
        s, hb = core // 2, core % 2
        r = R[core]
        y_prompt[s, hb * half:(hb + 1) * half] = r["yc"]
        y_sample[4 * core:4 * core + 4] = r["ysm"].reshape(4, 32, D)
        if hb == 1:
            conv_prompt[0, s] = r["convp"].transpose(2, 1, 0).reshape(3, 1536)
            ssm_prompt[0, s] = r["ssmp"].T.reshape(16, 64, 128)
        conv_sample[0, 4 * core:4 * core + 4] = r["convs"].transpose(2, 3, 1, 0).reshape(4, 3, 1536)
        ssm_sample[0, 4 * core:4 * core + 4] = r["ssms"].transpose(0, 2, 1).reshape(4, 16, 64, 128)
        vn[0, 4 * core:4 * core + 4] = r["vns"].reshape(4, 32, D)
    return (y_prompt, y_sample, conv_prompt, ssm_prompt, conv_sample, ssm_sample, vn)
```
